# Optimizing a Trainium2 kernel written in Bass

```python
import jax, jax.numpy as jnp
from jax import lax
import numpy as np

D_MODEL = 1024
BATCH = 1
SEQ = 16384
DEPTH = 2

N_A_LAYERS = DEPTH // 2
N_B_LAYERS = DEPTH - N_A_LAYERS

CONV_WIDTH = 31
CONV_CH = D_MODEL

HEAD_DIM = 64
HEADS_PER_GROUP = D_MODEL // HEAD_DIM
ATTN_WIDTH = HEADS_PER_GROUP * HEAD_DIM
DILATED_GROUPS = ((128, 1), (512, 4), (2048, 16))
N_GROUPS = len(DILATED_GROUPS)
Q_WIDTH = N_GROUPS * ATTN_WIDTH
BLOCK = 128
ALIBI_MAX_EXP = 8.0

ALPHA = (2.0 * DEPTH) ** 0.25
BETA = (8.0 * DEPTH) ** -0.25
LN_EPS = 1e-5

kernel_name = "yoco_conformer_conv_dilated_attn_deepnorm"


def layer_norm(x, g, b):
    xf = x.astype(jnp.float32)
    mu = jnp.mean(xf, axis=-1, keepdims=True)
    xc = xf - mu
    var = jnp.mean(xc * xc, axis=-1, keepdims=True)
    y = xc * lax.rsqrt(var + LN_EPS) * g.astype(jnp.float32) + b.astype(jnp.float32)
    return y.astype(x.dtype)


def alibi_slopes(n_heads):
    h = jnp.arange(1, n_heads + 1, dtype=jnp.float32)
    return jnp.exp2(-ALIBI_MAX_EXP * h / n_heads)


def conformer_conv_branch(x, w_in, b_in, w_dw, b_dw, ln_g, ln_b, w_out, b_out):
    h = x @ w_in + b_in
    a, a_gate, z = jnp.split(h, 3, axis=-1)
    u = a * jax.nn.sigmoid(a_gate)
    u = lax.conv_general_dilated(
        u, w_dw[:, None, :].astype(u.dtype), window_strides=(1,),
        padding=((CONV_WIDTH - 1, 0),),
        dimension_numbers=('NWC', 'WIO', 'NWC'),
        feature_group_count=CONV_CH) + b_dw
    u = jax.nn.silu(layer_norm(u, ln_g, ln_b))
    return (u * jax.nn.silu(z)) @ w_out + b_out


def dilated_window_attention(q, k, v, slopes, window, dilation):
    B, S, H, hd = q.shape
    n_back = window // dilation
    span = dilation * BLOCK
    s_pad = -(-S // span) * span
    L = s_pad // dilation
    nb = L // BLOCK
    pad = ((0, 0), (0, s_pad - S), (0, 0), (0, 0))

    def to_blocks(t):
        t = jnp.pad(t.astype(jnp.float32), pad).reshape(B, L, dilation, H, hd)
        return t.transpose(0, 2, 1, 3, 4).reshape(B, dilation, nb, BLOCK, H, hd)

    def with_prev(t):
        prev = jnp.pad(t[:, :, :-1], ((0, 0), (0, 0), (1, 0), (0, 0), (0, 0), (0, 0)))
        return jnp.concatenate([prev, t], axis=3)

    qb = to_blocks(q)
    kk = with_prev(to_blocks(k))
    vv = with_prev(to_blocks(v))

    scores = jnp.einsum('brnqhd,brnkhd->brnhqk', qb, kk) * (hd ** -0.5)
    qi = jnp.arange(BLOCK)[:, None]
    kj = jnp.arange(2 * BLOCK)[None, :]
    dist = qi + BLOCK - kj
    band = (dist >= 0) & (dist <= n_back)
    first = (jnp.arange(nb) == 0)[:, None, None]
    valid = band[None] & ~(first & (kj < BLOCK)[None])
    bias = -slopes[:, None, None] * (dilation * dist).astype(jnp.float32)[None]
    scores = jnp.where(valid[None, None, :, None], scores + bias, -jnp.inf)

    m = jnp.max(scores, axis=-1, keepdims=True)
    p = jnp.exp(scores - m)
    denom = jnp.sum(p, axis=-1, keepdims=True)
    o = jnp.einsum('brnhqk,brnkhd->brnqhd', p / denom, vv)
    lse = (m + jnp.log(denom))[..., 0]

    o = o.reshape(B, dilation, L, H, hd).transpose(0, 2, 1, 3, 4).reshape(B, s_pad, H, hd)[:, :S]
    lse = lse.transpose(0, 1, 2, 4, 3).reshape(B, dilation, L, H).transpose(0, 2, 1, 3)
    lse = lse.reshape(B, s_pad, H)[:, :S]
    return o, lse


def dilated_attention_branch(x, w_in, w_out, b_out, k_shared, v_shared):
    B, S, _ = x.shape
    h = x @ w_in
    q = h[..., :Q_WIDTH].reshape(B, S, N_GROUPS, HEADS_PER_GROUP, HEAD_DIM)
    z = h[..., Q_WIDTH:]
    slopes = alibi_slopes(HEADS_PER_GROUP)
    outs, lses = [], []
    for g, (window, dilation) in enumerate(DILATED_GROUPS):
        o, l = dilated_window_attention(q[:, :, g], k_shared[:, :, g], v_shared[:, :, g],
                                        slopes, window, dilation)
        outs.append(o)
        lses.append(l)
    wts = jax.nn.softmax(jnp.stack(lses, axis=0), axis=0)
    o = jnp.sum(wts[..., None] * jnp.stack(outs, axis=0), axis=0)
    o = o.reshape(B, S, ATTN_WIDTH).astype(x.dtype)
    return (o * jax.nn.silu(z)) @ w_out + b_out


def setup_inputs(seed: int = 0) -> dict:
    key = jax.random.key(seed)
    ks = jax.random.split(key, 16)
    f32 = jnp.float32
    nrm = lambda k, shape: jax.random.normal(k, shape, dtype=f32)
    C, D = CONV_CH, D_MODEL
    return {
        "x": nrm(ks[0], (BATCH, SEQ, D)),
        "a_w_in": nrm(ks[1], (N_A_LAYERS, D, 3 * C)) * D ** -0.5,
        "a_b_in": 0.02 * nrm(ks[2], (N_A_LAYERS, 3 * C)),
        "a_w_dw": nrm(ks[3], (N_A_LAYERS, CONV_WIDTH, C)) * CONV_WIDTH ** -0.5,
        "a_b_dw": 0.02 * nrm(ks[4], (N_A_LAYERS, C)),
        "a_ln_g": 1.0 + 0.02 * nrm(ks[5], (N_A_LAYERS, C)),
        "a_ln_b": 0.02 * nrm(ks[6], (N_A_LAYERS, C)),
        "a_w_out": nrm(ks[7], (N_A_LAYERS, C, D)) * (C ** -0.5 * BETA),
        "a_b_out": 0.02 * nrm(ks[8], (N_A_LAYERS, D)),
        "kv_w": nrm(ks[9], (D, 2 * Q_WIDTH)) * D ** -0.5,
        "b_w_in": nrm(ks[10], (N_B_LAYERS, D, Q_WIDTH + ATTN_WIDTH)) * D ** -0.5,
        "b_w_out": nrm(ks[11], (N_B_LAYERS, ATTN_WIDTH, D)) * (ATTN_WIDTH ** -0.5 * BETA),
        "b_b_out": 0.02 * nrm(ks[12], (N_B_LAYERS, D)),
        "post_ln_g": 1.0 + 0.02 * nrm(ks[13], (DEPTH, D)),
        "post_ln_b": 0.02 * nrm(ks[14], (DEPTH, D)),
    }


def reference(x, a_w_in, a_b_in, a_w_dw, a_b_dw, a_ln_g, a_ln_b, a_w_out, a_b_out,
              kv_w, b_w_in, b_w_out, b_b_out, post_ln_g, post_ln_b):
    B, S, _ = x.shape
    k_shared = v_shared = None
    for layer in range(DEPTH):
        if layer < N_A_LAYERS:
            i = layer
            y = conformer_conv_branch(x, a_w_in[i], a_b_in[i], a_w_dw[i], a_b_dw[i],
                                      a_ln_g[i], a_ln_b[i], a_w_out[i], a_b_out[i])
        else:
            if layer == N_A_LAYERS:
                kv = (x @ kv_w).reshape(B, S, 2, N_GROUPS, HEADS_PER_GROUP, HEAD_DIM)
                k_shared, v_shared = kv[:, :, 0], kv[:, :, 1]
            i = layer - N_A_LAYERS
            y = dilated_attention_branch(x, b_w_in[i], b_w_out[i], b_b_out[i], k_shared, v_shared)
        x = layer_norm(ALPHA * x + y, post_ln_g[layer], post_ln_b[layer])
    return x
```

```python
import numpy as np
from contextlib import ExitStack
import concourse.bass as bass
import concourse.mybir as mybir
from concourse.bass_utils import run_bass_kernel_spmd

F32 = mybir.dt.float32
BF16 = mybir.dt.bfloat16
AF = mybir.ActivationFunctionType
ALU = mybir.AluOpType

NCORES = 8
SEQ = 16384
D = 1024
OWN = 2048
HALO = 2048
PRE = 32
TA = 256
NTA = (HALO + OWN) // TA
XCOLS = PRE + HALO + OWN
ALPHA = 4.0 ** 0.25
LN_EPS = 1e-5
GROUPS = ((128, 1), (512, 4), (2048, 16))
BIG = 1.0e5
ENGS = ("pe", "act", "dve", "pool", "sp")
DEBUG = False


class Sched:
    def __init__(self, nc, sems):
        self.nc = nc
        self.sem = sems
        self.q = {e: [] for e in ENGS}
        self.cnt = {e: 0 for e in ENGS}
        self.seen = {e: {} for e in ENGS}
        self.last_w = {}
        self.readers = {}
        self.dma_sems = {}

    def _need(self, eng, tok, waits):
        if tok is None:
            return
        if tok[0] == "eng":
            _, pe, seq = tok
            if pe == eng and eng == "pe":
                return
            key = ("eng", pe)
        else:
            _, sn, seq = tok
            key = ("dma", sn)
        if self.seen[eng].get(key, 0) >= seq:
            return
        if waits.get(key, 0) < seq:
            waits[key] = seq

    def _deps(self, eng, reads, writes):
        waits = {}
        for r in reads:
            self._need(eng, self.last_w.get(r), waits)
        for w in writes:
            self._need(eng, self.last_w.get(w), waits)
            for t in self.readers.get(w, ()):
                if t[0] == "eng" and t[1] == eng and eng == "pe":
                    continue
                self._need(eng, t, waits)
        for key, v in waits.items():
            self.seen[eng][key] = v
        return list(waits.items())

    def op(self, eng, fn, reads=(), writes=(), mark=True):
        waits = self._deps(eng, reads, writes)
        if mark:
            self.cnt[eng] += 1
            tok = ("eng", eng, self.cnt[eng])
        else:
            tok = ("eng", eng, self.cnt[eng] + 1)
        self.q[eng].append(("op", fn, waits, mark))
        for w in writes:
            self.last_w[w] = tok
            self.readers[w] = []
        for r in reads:
            self.readers.setdefault(r, []).append(tok)
        return tok

    def dma(self, eng, semname, fn, reads=(), writes=()):
        waits = self._deps(eng, reads, writes)
        ent = self.dma_sems[semname]
        ent[1] += 16
        tok = ("dma", semname, ent[1])
        self.q[eng].append(("dma", fn, waits, semname))
        for w in writes:
            self.last_w[w] = tok
            self.readers[w] = []
        for r in reads:
            self.readers.setdefault(r, []).append(tok)
        return tok

    def barrier(self):
        for e in ENGS:
            waits = {}
            for p in ENGS:
                if p != e and self.cnt[p] > self.seen[e].get(("eng", p), 0):
                    waits[("eng", p)] = self.cnt[p]
            for sn, ent in self.dma_sems.items():
                if ent[1] > self.seen[e].get(("dma", sn), 0):
                    waits[("dma", sn)] = ent[1]
            for k, v in waits.items():
                self.seen[e][k] = v
            self.q[e].append(("wait", None, list(waits.items()), None))

    def replay(self, eng, e):
        for kind, fn, waits, extra in self.q[eng]:
            for key, v in waits:
                if key[0] == "eng":
                    e.wait_ge(self.sem[key[1]], v)
                else:
                    e.wait_ge(self.dma_sems[key[1]][0], v)
            if kind == "op":
                ins = fn(e)
                if extra:
                    ins.then_inc(self.sem[eng], 1)
            elif kind == "dma":
                fn(e).then_inc(self.dma_sems[extra][0], 16)


def I(name, **kw):
    return lambda e: getattr(e, name)(**kw)


def build_nc():
    nc = bass.Bass("TRN2", target_bir_lowering=False)

    def din(name, shape):
        return nc.dram_tensor(name, list(shape), F32, kind="ExternalInput").ap()

    xT = din("xT", [8, 128, XCOLS])
    a_w_in = din("a_w_in", [8, 128, 3072])
    a_w_out = din("a_w_out", [8, 128, 1024])
    kv_w = din("kv_w", [8, 128, 6144])
    b_w_in = din("b_w_in", [8, 128, 4096])
    b_w_out = din("b_w_out", [8, 128, 1024])
    cvec = din("cvec", [128, 96])
    wdw = din("wdw", [128, 8 * 31])
    hm = din("hm", [128, 2])
    ident = din("ident", [128, 128])
    dist3 = din("dist3", [128, 512])
    outT = nc.dram_tensor("outT", [8, 128, OWN], F32, kind="ExternalOutput").ap()
    x1f = nc.dram_tensor("x1f", [8, 128, OWN], F32, kind="ExternalOutput" if DEBUG else "Internal").ap()
    x1T_d = nc.dram_tensor("x1T_d", [8, 128, HALO + OWN], BF16, kind="Internal").ap()

    es = ExitStack()
    with es:
        def sb(name, shape, dt):
            return es.enter_context(nc.sbuf_tensor(name, list(shape), dt))

        NEL = 103936
        arena = sb("arena", [128, NEL], BF16)

        def AR(off, nbytes, dt=BF16):
            assert off % 4 == 0 and (off + nbytes) <= NEL * 2, (off, nbytes)
            ap = arena[:, off // 2:(off + nbytes) // 2]
            return ap.bitcast(F32) if dt == F32 else ap

        def K8(ap):
            return ap.rearrange("p (k n) -> p k n", k=8)

        cv_t = sb("cvec_t", [128, 96], F32)
        wdw_t = sb("wdw_t", [128, 8 * 31], F32)
        hm_t = sb("hm_t", [128, 2], F32)
        ident_t = sb("ident_t", [128, 128], F32)
        dist3_t = sb("dist3_t", [128, 512], F32)
        onesm = sb("onesm", [128, 128], BF16)
        onesf = sb("onesf", [128, 64], F32)
        banks = [es.enter_context(nc.psum_tensor("bank%d" % i, [128, 512], F32)) for i in range(8)]

        sems = {e: es.enter_context(nc.semaphore("s_" + e)) for e in ENGS}
        S = Sched(nc, sems)
        NDS = {"sp": 40, "pool": 24}
        for q_, n_ in NDS.items():
            for i in range(n_):
                S.dma_sems["%s%d" % (q_, i)] = [es.enter_context(nc.semaphore("d%s%d" % (q_, i))), 0]
        dsi = {"sp": 0, "pool": 0}

        def DMA(eng, fn, reads=(), writes=()):
            sn = "%s%d" % (eng, dsi[eng] % NDS[eng])
            dsi[eng] += 1
            prev = S.dma_sems[sn][1]
            if prev > S.seen[eng].get(("dma", sn), 0):
                S.q[eng].append(("wait", None, [(("dma", sn), prev)], None))
                S.seen[eng][("dma", sn)] = prev
            return S.dma(eng, sn, fn, reads=reads, writes=writes)

        rr = {}

        def nbank(lo, hi):
            k = (lo, hi)
            i = lo + rr.get(k, 0) % (hi - lo)
            rr[k] = rr.get(k, 0) + 1
            return i

        C_BIN, C_BDW, C_LNG, C_LNB, C_BOUT, C_PG0, C_PB0, C_BBOUT, C_PG1, C_PB1 = 0, 24, 32, 40, 48, 56, 64, 72, 80, 88

        def col(c0, i):
            return cv_t[:, c0 + i:c0 + i + 1]

        def mm(out, lhsT, rhs, start, stop, reads, writes):
            S.op("pe", I("matmul", out=out, lhsT=lhsT, rhs=rhs, start=start, stop=stop), reads=reads, writes=writes, mark=stop)

        for t, src, nm in ((cv_t, cvec, "cvec"), (wdw_t, wdw, "wdw"), (hm_t, hm, "hm"), (ident_t, ident, "ident"), (dist3_t, dist3, "dist3")):
            DMA("sp", I("dma_start", out=t[:], in_=src), writes=[nm])
        S.op("dve", I("memset", ap=onesm[:], constant=1.0 / 1024.0), writes=["onesm"])
        S.op("dve", I("memset", ap=onesf[:], constant=1.0), writes=["onesf"])

        o = 0
        NDT = 6
        NPT = 31 - NDT
        diag = AR(o, NPT * 8 * 256).rearrange("p (j c m) -> p j c m", j=NPT, c=8); o += NPT * 8 * 256
        winb = K8(AR(o, 8 * 3072 * 2)); o += 8 * 3072 * 2
        woutb = K8(AR(o, 8 * 1024 * 2)); o += 8 * 1024 * 2
        xt = K8(AR(o, 8 * TA * 4, F32)); o += 8 * TA * 4
        xb = [K8(AR(o + i * 8 * TA * 2, 8 * TA * 2)) for i in range(2)]; o += 2 * 8 * TA * 2
        UW = PRE + TA
        ub = [K8(AR(o + i * 8 * UW * 2, 8 * UW * 2)) for i in range(2)]; o += 2 * 8 * UW * 2
        sg = [AR(o + i * TA * 4, TA * 4, F32) for i in range(2)]; o += 2 * TA * 4
        szb = [K8(AR(o + i * 8 * TA * 2, 8 * TA * 2)) for i in range(2)]; o += 2 * 8 * TA * 2
        AR_cvf0, AR_cvf1 = AR(o, 2048, F32), AR(o + 4096, 2048, F32)
        cvf = K8(AR(o, 8 * TA * 4, F32)); o += 8 * TA * 4
        cvb = K8(AR(o, 8 * TA * 2)); o += 8 * TA * 2
        sqb = K8(AR(o, 8 * TA * 2)); o += 8 * TA * 2
        rbb, rsq = cvb, sqb
        st1 = [AR(o + i * TA * 8, TA * 8, F32) for i in range(2)]; o += 2 * TA * 8
        st2 = [AR(o + i * TA * 8, TA * 8, F32) for i in range(2)]; o += 2 * TA * 8
        tt = [AR(o + i * TA * 8, TA * 8, F32) for i in range(4)]; o += 4 * TA * 8
        yy = [AR(o + i * TA * 4, TA * 4).rearrange("p (k n) -> p k n", k=2) for i in range(4)]; o += 4 * TA * 4
        vbs = [K8(AR(o + i * 8 * TA * 2, 8 * TA * 2)) for i in range(2)]; o += 2 * 8 * TA * 2
        rf = xt
        assert o <= NEL * 2, o

        def ln_stats(srcb, sqsrc, n, r_src, r_sq, bufs, tag, bm, be):
            mean_sb, rstd_sb = bufs
            for rep in range(2):
                for c in range(8):
                    mm(banks[bm][:, rep * n:(rep + 1) * n], onesm[:], srcb[:, c, 0:n], c == 0, c == 7, ["onesm", r_src], ["bank%d" % bm])
            for c in range(8):
                mm(banks[be][:, 0:n], onesm[:], sqsrc[:, c, 0:n], c == 0, c == 7, ["onesm", r_sq], ["bank%d" % be])
            n2 = 2 * n
            S.op("act", I("activation", out=rstd_sb[:, 0:n], in_=banks[bm][:, 0:n], func=AF.Square), reads=["bank%d" % bm], writes=[tag + "rstd"])
            S.op("act", I("activation", out=mean_sb[:, 0:n2], in_=banks[bm][:, 0:n2], func=AF.Copy), reads=["bank%d" % bm], writes=[tag + "mean"])
            yield
            S.op("dve", I("scalar_tensor_tensor", out=rstd_sb[:, 0:n], in0=rstd_sb[:, 0:n], scalar=-1.0, in1=banks[be][:, 0:n], op0=ALU.mult, op1=ALU.add),
                 reads=["bank%d" % be, tag + "rstd"], writes=[tag + "rstd"])
            S.op("act", I("activation", out=rstd_sb[:, 0:n], in_=rstd_sb[:, 0:n], func=AF.Sqrt, bias=LN_EPS, scale=1.0), reads=[tag + "rstd"], writes=[tag + "rstd"])
            yield
            S.op("dve", I("reciprocal", out=rstd_sb[:, 0:n], in_=rstd_sb[:, 0:n]), reads=[tag + "rstd"], writes=[tag + "rstd"])
            S.op("dve", I("tensor_copy", out=rstd_sb[:, n:n2], in_=rstd_sb[:, 0:n]), reads=[tag + "rstd"], writes=[tag + "rstd"])

        def build_diag(c):
            for jj in range(NPT):
                j = NDT + jj
                sel = 1 if jj % 3 == 2 else 0
                res = "diag%d_%d" % (c, sel)
                if sel == 0:
                    S.op("dve", I("tensor_scalar", out=diag[:, jj, c, :], in0=ident_t[:], scalar1=wdw_t[:, c * 31 + j:c * 31 + j + 1], scalar2=None, op0=ALU.mult),
                         reads=["ident", "wdw"], writes=[res])
                else:
                    S.op("act", I("activation", out=diag[:, jj, c, :], in_=ident_t[:], func=AF.Copy, scale=wdw_t[:, c * 31 + j:c * 31 + j + 1]),
                         reads=["ident", "wdw"], writes=[res])

        def diag_gen():
            for c in range(4, 8):
                yield
                build_diag(c)
                yield

        stg = [(AR_cvf0, ["cvf0", "cvf1"]), (AR_cvf1, ["cvf4", "cvf5"]), (vbs[0].rearrange("p k n -> p (k n)").bitcast(F32), ["vb0"]),
               (vbs[1].rearrange("p k n -> p (k n)").bitcast(F32), ["vb1"]), (st1[0], ["s1mean"]), (st1[1], ["s1rstd"]),
               (st2[0], ["s2mean"]), (st2[1], ["s2rstd"]), (tt[0], ["tt0"]), (tt[1], ["tt1"])]
        stg_i = {"i": 0}

        def load_piece(dst3, src3, dst_res, n_k):
            w = dst3.shape[2]
            per = max(1, 512 // w)
            for k0 in range(0, n_k, per):
                kk = min(per, n_k - k0)
                i = stg_i["i"] % len(stg)
                stg_i["i"] += 1
                buf, names = stg[i]
                bv = buf[:, 0:kk * w].rearrange("p (k n) -> p k n", k=kk)
                DMA("sp", I("dma_start", out=bv, in_=src3[:, k0:k0 + kk, :]), writes=["stg%d" % i])
                if i % 2:
                    S.op("act", I("activation", out=dst3[:, k0:k0 + kk, :], in_=bv, func=AF.Copy), reads=["stg%d" % i] + names, writes=[dst_res])
                else:
                    S.op("dve", I("tensor_copy", out=dst3[:, k0:k0 + kk, :], in_=bv), reads=["stg%d" % i] + names, writes=[dst_res])

        def load_win_piece(part, c):
            cs = slice(part * 1024 + c * 128, part * 1024 + (c + 1) * 128)
            load_piece(winb[:, :, cs], a_w_in[:, :, cs].rearrange("k p n -> p k n"), "winb%d_%d" % (part, c), 8)

        def load_x(ti):
            if ti < 0:
                DMA("pool", I("dma_start", out=xb[1][:, :, 0:PRE], in_=xT[:, :, 0:PRE].rearrange("k p n -> p k n")), writes=["xb1"])
            else:
                c0 = PRE + ti * TA
                DMA("pool", I("dma_start", out=xb[ti % 2][:, :, :], in_=xT[:, :, c0:c0 + TA].rearrange("k p n -> p k n")), writes=["xb%d" % (ti % 2)])

        load_x(-1)
        load_x(0)

        def late_weights():
            for c in range(8):
                load_win_piece(2, c)
                yield
            for k in range(8):
                load_piece(woutb[:, k, :].rearrange("p (a n) -> p a n", a=2), a_w_out[k].rearrange("p (a n) -> p a n", a=2), "woutb", 2)
                yield

        diag_done = {"c": 0}

        def glu_chunk(c, n, xsrc, xres_, udst, ures, ucol0):
            bg = nbank(0, 6)
            for k in range(8):
                mm(banks[bg][:, 0:n], winb[:, k, 1024 + c * 128:1024 + (c + 1) * 128], xsrc[:, k, 0:n], k == 0, k == 7, ["winb1_%d" % c, xres_], ["bank%d" % bg])
            s = sg[c % 2]
            S.op("act", I("activation", out=s[:, 0:n], in_=banks[bg][:, 0:n], func=AF.Sigmoid, bias=col(C_BIN, 8 + c), scale=1.0),
                 reads=["bank%d" % bg, "cvec"], writes=["sg%d" % (c % 2)])
            ba = nbank(0, 6)
            for k in range(8):
                mm(banks[ba][:, 0:n], winb[:, k, c * 128:(c + 1) * 128], xsrc[:, k, 0:n], k == 0, k == 7, ["winb0_%d" % c, xres_], ["bank%d" % ba])
            S.op("dve", I("scalar_tensor_tensor", out=udst[:, c, ucol0:ucol0 + n], in0=banks[ba][:, 0:n], scalar=col(C_BIN, c), in1=s[:, 0:n], op0=ALU.add, op1=ALU.mult),
                 reads=["bank%d" % ba, "sg%d" % (c % 2), "cvec"], writes=[ures])

        for c in range(8):
            for part in (1, 0):
                load_win_piece(part, c)
            glu_chunk(c, PRE, xb[1], "xb1", ub[0], "u0", 0)
            if c % 2 == 1:
                build_diag(c // 2)
        S.op("pool", I("tensor_scalar", out=ub[0][:, :, 0:PRE], in0=ub[0][:, :, 0:PRE], scalar1=hm_t[:, 0:1], scalar2=None, op0=ALU.mult),
             reads=["u0", "hm"], writes=["u0"])

        def stage_Ag(ti):
            p = ti % 2
            U, Un = ub[p], ub[1 - p]
            X = xb[p]
            if ti + 1 < NTA:
                load_x(ti + 1)
            for c in range(8):
                glu_chunk(c, TA, X, "xb%d" % p, U, "u%d" % p, PRE)
                yield
            if ti + 1 < NTA:
                if ti + 1 == HALO // TA:
                    S.op("pool", I("tensor_scalar", out=Un[:, :, 0:PRE], in0=U[:, :, TA:TA + PRE], scalar1=hm_t[:, 1:2], scalar2=None, op0=ALU.mult),
                         reads=["u%d" % p, "hm"], writes=["u%d" % (1 - p)])
                else:
                    S.op("pool", I("tensor_copy", out=Un[:, :, 0:PRE], in_=U[:, :, TA:TA + PRE]), reads=["u%d" % p], writes=["u%d" % (1 - p)])

        def stage_Az(ti):
            p = ti % 2
            X = xb[p]
            for c in range(8):
                bz = nbank(0, 6)
                for k in range(8):
                    mm(banks[bz][:, 0:TA], winb[:, k, 2048 + c * 128:2048 + (c + 1) * 128], X[:, k, :], k == 0, k == 7, ["winb2_%d" % c, "xb%d" % p], ["bank%d" % bz])
                S.op("act", I("activation", out=szb[p][:, c, :], in_=banks[bz][:, 0:TA], func=AF.Silu, bias=col(C_BIN, 16 + c), scale=1.0),
                     reads=["bank%d" % bz, "cvec"], writes=["szb%d" % p])
                yield

        def stage_C(ti):
            p = ti % 2
            U = ub[p]
            for c in range(8):
                bc = nbank(0, 6)
                for jj in range(NPT):
                    j = NDT + jj
                    mm(banks[bc][:, 0:TA], diag[:, jj, c, :], U[:, c, 2 + j:2 + j + TA], jj == 0, jj == NPT - 1,
                       ["diag%d_%d" % (c, 1 if jj % 3 == 2 else 0), "u%d" % p], ["bank%d" % bc])
                S.op("act", I("activation", out=cvf[:, c, :], in_=banks[bc][:, 0:TA], func=AF.Identity, bias=col(C_BDW, c), scale=1.0),
                     reads=["bank%d" % bc, "cvec"], writes=["cvf%d" % c])
                for j in range(NDT):
                    S.op("dve", I("scalar_tensor_tensor", out=cvf[:, c, :], in0=U[:, c, 2 + j:2 + j + TA], scalar=wdw_t[:, c * 31 + j:c * 31 + j + 1], in1=cvf[:, c, :],
                                  op0=ALU.mult, op1=ALU.add),
                         reads=["u%d" % p, "wdw", "cvf%d" % c], writes=["cvf%d" % c])
            allc = ["cvf%d" % c for c in range(8)]
            S.op("act", I("activation", out=sqb[:, :, :], in_=cvf[:, :, :], func=AF.Square), reads=allc, writes=["sqb"])
            S.op("dve", I("tensor_copy", out=cvb[:, :, :], in_=cvf[:, :, :]), reads=allc, writes=["cvb"])

        def stage_S1(ti):
            p = ti % 2
            yield
            yield
            yield from ln_stats(cvb, sqb, TA, "cvb", "sqb", st1, "s1", 6, 7)
            mean_sb, rstd_sb = st1
            yield

            def gate(c2):
                y = yy[c2 % 4]
                S.op("dve", I("tensor_tensor", out=vbs[p][:, 2 * c2:2 * c2 + 2, :], in0=y[:, :, :], in1=szb[p][:, 2 * c2:2 * c2 + 2, :], op=ALU.mult),
                     reads=["yy%d" % (c2 % 4), "szb%d" % p], writes=["vb%d" % p])

            for c2 in range(4):
                t, y = tt[c2 % 4], yy[c2 % 4]
                tv = t.rearrange("p (k n) -> p k n", k=2)
                S.op("dve", I("tensor_tensor", out=tv, in0=cvf[:, 2 * c2:2 * c2 + 2, :], in1=mean_sb.rearrange("p (k n) -> p k n", k=2), op=ALU.subtract),
                     reads=["cvf%d" % (2 * c2), "cvf%d" % (2 * c2 + 1), "s1mean"], writes=["tt%d" % (c2 % 4)])
                S.op("dve", I("tensor_tensor", out=t[:], in0=t[:], in1=rstd_sb[:], op=ALU.mult), reads=["tt%d" % (c2 % 4), "s1rstd"], writes=["tt%d" % (c2 % 4)])
                for j in range(2):
                    c = 2 * c2 + j
                    S.op("act", I("activation", out=y[:, j, :], in_=tv[:, j, :], func=AF.Silu, bias=col(C_LNB, c), scale=col(C_LNG, c)),
                         reads=["tt%d" % (c2 % 4), "cvec"], writes=["yy%d" % (c2 % 4)])
                if c2 >= 1:
                    gate(c2 - 1)
                yield
            gate(3)

        RF = ["rf0", "rf1", "rf2", "rf3"]

        def stage_O(ti):
            c0 = PRE + ti * TA
            DMA("sp", I("dma_start", out=xt[:, :, :], in_=xT[:, :, c0:c0 + TA].rearrange("k p n -> p k n")), writes=RF)
            S.op("dve", I("tensor_scalar", out=xt[:, :, :], in0=xt[:, :, :], scalar1=ALPHA, scalar2=None, op0=ALU.mult), reads=RF, writes=RF)
            for m in range(8):
                bo = nbank(0, 6)
                for c in range(8):
                    mm(banks[bo][:, 0:TA], woutb[:, c, m * 128:(m + 1) * 128], vbs[ti % 2][:, c, :], c == 0, c == 7, ["woutb", "vb%d" % (ti % 2)], ["bank%d" % bo])
                S.op("dve", I("scalar_tensor_tensor", out=rf[:, m, :], in0=banks[bo][:, 0:TA], scalar=col(C_BOUT, m), in1=xt[:, m, :], op0=ALU.add, op1=ALU.add),
                     reads=["bank%d" % bo, "rf%d" % (m // 2), "cvec"], writes=["rf%d" % (m // 2)])
                yield
            S.op("dve", I("tensor_copy", out=rbb[:, :, :], in_=rf[:, :, :]), reads=RF, writes=["cvb"])
            S.op("act", I("activation", out=rsq[:, :, :], in_=rf[:, :, :], func=AF.Square), reads=RF, writes=["sqb"])

        def stage_S2(ti):
            own = ti >= HALO // TA
            yield from ln_stats(rbb, rsq, TA, "cvb", "sqb", st2, "s2", 6, 7)
            mean_sb, rstd_sb = st2
            yield
            def affine(c2):
                tv = tt[c2 % 4].rearrange("p (k n) -> p k n", k=2)
                for j in range(2):
                    m = 2 * c2 + j
                    dst = rf[:, m, :] if own else rbb[:, m, :]
                    S.op("act", I("activation", out=dst, in_=tv[:, j, :], func=AF.Identity, bias=col(C_PB0, m), scale=col(C_PG0, m)),
                         reads=["tt%d" % (c2 % 4), "cvec"], writes=["rf%d" % c2 if own else "cvb"])

            for c2 in range(4):
                t = tt[c2 % 4]
                tv = t.rearrange("p (k n) -> p k n", k=2)
                S.op("dve", I("tensor_tensor", out=tv, in0=rf[:, 2 * c2:2 * c2 + 2, :], in1=mean_sb.rearrange("p (k n) -> p k n", k=2), op=ALU.subtract),
                     reads=["rf%d" % c2, "s2mean"], writes=["tt%d" % (c2 % 4)])
                S.op("dve", I("tensor_tensor", out=t[:], in0=t[:], in1=rstd_sb[:], op=ALU.mult), reads=["tt%d" % (c2 % 4), "s2rstd"], writes=["tt%d" % (c2 % 4)])
                if c2 >= 1:
                    affine(c2 - 1)
                yield
            affine(3)
            yield
            pos0 = ti * TA
            if own:
                S.op("dve", I("tensor_copy", out=rbb[:, :, :], in_=rf[:, :, :]), reads=RF, writes=["cvb"])
                p1 = pos0 - HALO
                DMA("sp", I("dma_start", out=x1f[:, :, p1:p1 + TA].rearrange("k p n -> p k n"), in_=rf[:, :, :]), reads=RF, writes=["x1f"])
            DMA("sp", I("dma_start", out=x1T_d[:, :, pos0:pos0 + TA].rearrange("k p n -> p k n"), in_=rbb[:, :, :]), reads=["cvb"], writes=["x1T_d"])

        def run(*gens):
            gens = list(gens)
            while gens:
                for g in list(gens):
                    try:
                        next(g)
                    except StopIteration:
                        gens.remove(g)

        run(stage_Ag(0), late_weights(), diag_gen())
        run(stage_Az(0))
        for i in range(NTA + 1):
            if i < NTA:
                stage_C(i)
            g = []
            if i >= 1:
                g.append(stage_O(i - 1))
            if i < NTA:
                g.append(stage_S1(i))
            if i + 1 < NTA:
                g.append(stage_Az(i + 1))
            run(*g)
            g = []
            if i + 1 < NTA:
                g.append(stage_Ag(i + 1))
            if i >= 1:
                g.append(stage_S2(i - 1))
            run(*g)

        S.barrier()

        NT = HALO + OWN
        o = 0
        x1T = K8(AR(o, 8 * NT * 2)); o += 8 * NT * 2
        gT = K8(AR(o, 8 * OWN * 2)); o += 8 * OWN * 2
        acc = [AR(o + i * OWN * 4, OWN * 4, F32) for i in range(2)]; o += 2 * OWN * 4
        szhs = [AR(o + i * OWN * 4, OWN * 4, F32) for i in range(2)]; o += 2 * OWN * 4
        o_attn = o
        wsl = [AR(o + i * 20480, 20480).rearrange("p (s k n) -> p s k n", s=10, k=8) for i in range(2)]; o += 2 * 20480
        qd = AR(o, OWN * 2); o += OWN * 2
        kd = AR(o, NT * 2); o += NT * 2
        vaug = AR(o, 32 * 2 * 65 * 2).rearrange("p (b h e) -> p b h e", b=32, h=2); o += 32 * 2 * 65 * 2
        D2 = AR(o, 2 * 2 * 512 * 2).rearrange("p (h v n) -> p h v n", h=2, v=2); o += 2 * 2 * 512 * 2
        eb = [AR(o + i * 1024, 1024) for i in range(2)]; o += 2 * 1024
        pb = [AR(o + i * 1024, 1024) for i in range(3)]; o += 3 * 1024
        tnb = AR(o, 512 * 4, F32); o += 512 * 4
        omax = o
        o = o_attn
        woutB = K8(AR(o, 8 * 1024 * 2)); o += 8 * 1024 * 2
        xall = K8(AR(0, 8 * OWN * 4, F32))
        rB = [K8(AR(o + i * 8 * TA * 4, 8 * TA * 4, F32)) for i in range(3)]; o += 3 * 8 * TA * 4
        rBb = [K8(AR(o + i * 8 * TA * 2, 8 * TA * 2)) for i in range(2)]; o += 2 * 8 * TA * 2
        rBq = [K8(AR(o + i * 8 * TA * 2, 8 * TA * 2)) for i in range(2)]; o += 2 * 8 * TA * 2
        st3 = [AR(o + i * TA * 8, TA * 8, F32) for i in range(2)]; o += 2 * TA * 8
        o_dead = 8 * NT * 2 + 8 * OWN * 2
        ttB = [AR(o_dead + i * TA * 8, TA * 8, F32) for i in range(4)]
        stgB = [AR(o_dead + 4 * TA * 8 + i * 2048, 2048, F32) for i in range(8)]
        assert max(o, omax) <= NEL * 2, (o, omax)

        for k in range(8):
            pass
        def xblocks(a0, n, step=1):
            last = a0 + (n - 1) * step
            return ["x1T_b%d" % b for b in range(a0 // 1024, last // 1024 + 1)]
        for blk in (2, 3, 0, 1):
            cs_ = slice(blk * 1024, (blk + 1) * 1024)
            DMA("sp", I("dma_start", out=x1T[:, :, cs_], in_=x1T_d[:, :, cs_].rearrange("k p n -> p k n")), reads=["x1T_d"], writes=["x1T_b%d" % blk])
        S.op("dve", I("memset", ap=vaug[:, :, :, 64:65], constant=1.0), writes=["vaug"])

        slopes = [2.0 ** (-8.0 * (h + 1) / 16.0) for h in range(16)]

        def load_w(hp):
            W = wsl[hp % 2]
            srcs = []
            for g in range(3):
                srcs.append((b_w_in, g * 1024 + hp * 128))
            for g in range(3):
                srcs.append((kv_w, g * 1024 + hp * 128))
            for g in range(3):
                srcs.append((kv_w, 3072 + g * 1024 + hp * 128))
            srcs.append((b_w_in, 3072 + hp * 128))
            order = [9, 0, 3, 6, 1, 4, 7, 2, 5, 8]
            for s in order:
                wsrc, c0 = srcs[s]
                DMA("pool", I("dma_start", out=W[:, s, :, :], in_=wsrc[:, :, c0:c0 + 128].rearrange("k p n -> p k n")), writes=["wsl%d_%d" % (hp % 2, s)])

        def norm_a(hp):
            for h2 in range(2):
                A = acc[h2]
                S.op("act", I("activation", out=A[64:65, :], in_=A[64:65, :], func=AF.Ln), reads=["acc%d" % h2], writes=["acc%d" % h2])
            for h2 in range(2):
                A = acc[h2]
                S.op("act", I("activation", out=A[64:65, :], in_=A[64:65, :], func=AF.Exp, scale=-1.0), reads=["acc%d" % h2], writes=["acc%d" % h2])

        def norm_b(hp):
            szh = szhs[hp % 2]
            for h2 in range(2):
                A = acc[h2]
                pl = slice(64 * h2, 64 * h2 + 64)
                for t4 in range(OWN // 512):
                    cs = slice(t4 * 512, (t4 + 1) * 512)
                    bb = nbank(0, 3)
                    S.op("pe", I("matmul", out=banks[bb][0:64, :], lhsT=onesf[64:65, 0:64], rhs=A[64:65, cs], start=True, stop=True),
                         reads=["onesf", "acc%d" % h2], writes=["bank%d" % bb])
                    S.op("dve", I("tensor_tensor", out=tnb[pl, :], in0=A[0:64, cs], in1=banks[bb][0:64, :], op=ALU.mult), reads=["acc%d" % h2, "bank%d" % bb], writes=["tnb"])
                    S.op("dve", I("tensor_tensor", out=gT[pl, hp, cs], in0=tnb[pl, :], in1=szh[pl, cs], op=ALU.mult), reads=["tnb", "szh%d" % (hp % 2)], writes=["gT"])

        load_w(0)
        for hp in range(8):
            W = wsl[hp % 2]
            wr = lambda s: "wsl%d_%d" % (hp % 2, s)
            for t4 in range(OWN // 512):
                bz = nbank(0, 3)
                for k in range(8):
                    mm(banks[bz][:, :], W[:, 9, k, :], x1T[:, k, HALO + t4 * 512:HALO + (t4 + 1) * 512], k == 0, k == 7, [wr(9)] + xblocks(HALO + t4 * 512, 512), ["bank%d" % bz])
                S.op("act", I("activation", out=szhs[hp % 2][:, t4 * 512:(t4 + 1) * 512], in_=banks[bz][:, :], func=AF.Silu), reads=["bank%d" % bz], writes=["szh%d" % (hp % 2)])
            if hp + 1 < 8:
                load_w(hp + 1)
            for g, (window, d) in enumerate(GROUPS):
                halo = 128 * d
                nq = 16 // d
                qv = qd.rearrange("p (r n i) -> p r n i", r=d, n=nq)
                kv_ = kd[:, 0:d * (nq + 1) * 128].rearrange("p (r n i) -> p r n i", r=d, n=nq + 1)
                for t4 in range(OWN // 512):
                    bq = nbank(0, 3)
                    for k in range(8):
                        mm(banks[bq][:, :], W[:, g, k, :], x1T[:, k, HALO + t4 * 512:HALO + (t4 + 1) * 512], k == 0, k == 7, [wr(g)] + xblocks(HALO + t4 * 512, 512), ["bank%d" % bq])
                    L = 512 // d
                    l0 = t4 * L
                    nb_, i0 = l0 // 128, l0 % 128
                    if d == 1:
                        dst, src = qd[:, t4 * 512:(t4 + 1) * 512], banks[bq][:, :]
                    else:
                        dst, src = qv[:, :, nb_, i0:i0 + L], banks[bq][:, :].rearrange("p (l r) -> p r l", r=d)
                    S.op("act", I("activation", out=dst, in_=src, func=AF.Copy, scale=0.125), reads=["bank%d" % bq], writes=["qd"])
                ntok = halo + OWN
                for t0 in range(0, ntok, 512):
                    n = min(512, ntok - t0)
                    bk = nbank(0, 3)
                    a0 = HALO - halo + t0
                    for k in range(8):
                        mm(banks[bk][:, 0:n], W[:, 3 + g, k, :], x1T[:, k, a0:a0 + n], k == 0, k == 7, [wr(3 + g)] + xblocks(a0, n), ["bank%d" % bk])
                    L = n // d
                    l0 = t0 // d
                    nb_, i0 = l0 // 128, l0 % 128
                    if d == 1:
                        dst, src = kd[:, t0:t0 + n], banks[bk][:, 0:n]
                    else:
                        dst, src = kv_[:, :, nb_, i0:i0 + L], banks[bk][:, 0:n].rearrange("p (l r) -> p r l", r=d)
                    S.op("dve", I("tensor_copy", out=dst, in_=src), reads=["bank%d" % bk], writes=["kd"])
                nblk = d * (nq + 1)
                for blk in range(nblk):
                    r, b = blk // (nq + 1), blk % (nq + 1)
                    bv = nbank(0, 3)
                    a0 = HALO - halo + d * b * 128 + r
                    for k in range(8):
                        mm(banks[bv][:, 0:128], x1T[:, k, a0:a0 + 127 * d + 1:d], W[:, 6 + g, k, :], k == 0, k == 7, [wr(6 + g)] + xblocks(a0, 128, d), ["bank%d" % bv])
                    src = banks[bv][:, 0:128].rearrange("p (h e) -> p h e", h=2)
                    if blk % 3 == 2:
                        S.op("act", I("activation", out=vaug[:, blk, :, 0:64], in_=src, func=AF.Copy), reads=["bank%d" % bv], writes=["vaug"])
                    else:
                        S.op("dve", I("tensor_copy", out=vaug[:, blk, :, 0:64], in_=src), reads=["bank%d" % bv], writes=["vaug"])
                for h2 in range(2):
                    sc = -slopes[hp * 2 + h2] * d
                    df_, dn_ = dist3_t[:, 0:256], dist3_t[:, 256:512]
                    S.op("act", I("activation", out=D2[:, h2, 0, 0:256], in_=df_, func=AF.Exp, scale=sc), reads=["dist3"], writes=["D2"])
                    S.op("act", I("activation", out=D2[:, h2, 0, 256:512], in_=(df_ if d == 16 else dn_), func=AF.Exp, scale=sc), reads=["dist3"], writes=["D2"])
                    if d != 16:
                        S.op("act", I("activation", out=D2[:, h2, 1, 0:256], in_=dn_, func=AF.Exp, scale=sc), reads=["dist3"], writes=["D2"])
                        S.op("act", I("activation", out=D2[:, h2, 1, 256:512], in_=dn_, func=AF.Exp, scale=sc), reads=["dist3"], writes=["D2"])
                if g == 0 and hp >= 1:
                    norm_b(hp - 1)
                pairs = [(h2, P) for h2 in range(2) for P in range(8)]

                def kq(h2, u):
                    pl = slice(64 * h2, 64 * h2 + 64)
                    r, n = u // nq, u % nq
                    if d == 1:
                        return kd[pl, n * 128:(n + 1) * 128], kd[pl, (n + 1) * 128:(n + 2) * 128], qd[pl, n * 128:(n + 1) * 128], n
                    return kv_[pl, r, n, :], kv_[pl, r, n + 1, :], qv[pl, r, n, :], r * (nq + 1) + n

                def emit_qk(idx):
                    h2, P = pairs[idx]
                    bs = 3 + idx % 3
                    for j in range(2):
                        kp, kc, q, _ = kq(h2, 2 * P + j)
                        mm(banks[bs][:, j * 256:j * 256 + 128], kp, q, True, True, ["kd", "qd"], ["bank%d" % bs])
                        mm(banks[bs][:, j * 256 + 128:j * 256 + 256], kc, q, True, True, ["kd", "qd"], ["bank%d" % bs])
                    E, Pb = eb[idx % 2], pb[idx % 3]
                    S.op("act", I("activation", out=E[:, :], in_=banks[bs][:, :], func=AF.Exp), reads=["bank%d" % bs], writes=["eb%d" % (idx % 2)])
                    var = 0 if (d == 16 or (2 * P) % nq == 0) else 1
                    S.op("dve", I("tensor_tensor", out=Pb[:, :], in0=E[:, :], in1=D2[:, h2, var, :], op=ALU.mult), reads=["eb%d" % (idx % 2), "D2"], writes=["pb%d" % (idx % 3)])

                def emit_pv(idx):
                    h2, P = pairs[idx]
                    Pb = pb[idx % 3]
                    bo = 6 + (P // 2) % 2
                    A = acc[h2]
                    for j in range(2):
                        u = 2 * P + j
                        _, _, _, blk0 = kq(h2, u)
                        oc = (u % 4) * 128
                        mm(banks[bo][0:65, oc:oc + 128], vaug[:, blk0, h2, :], Pb[:, j * 256:j * 256 + 128], True, False, ["vaug", "pb%d" % (idx % 3)], ["bank%d" % bo])
                        mm(banks[bo][0:65, oc:oc + 128], vaug[:, blk0 + 1, h2, :], Pb[:, j * 256 + 128:j * 256 + 256], False, True, ["vaug", "pb%d" % (idx % 3)], ["bank%d" % bo])
                    if P % 2 == 1:
                        k4 = P // 2
                        if d == 1:
                            av, pv = A[0:65, k4 * 512:(k4 + 1) * 512], banks[bo][0:65, :]
                        elif d == 4:
                            av, pv = A[0:65, :].rearrange("p (l r) -> p r l", r=4)[:, k4, :], banks[bo][0:65, :]
                        else:
                            av = A[0:65, :].rearrange("p (i r) -> p r i", r=16)[:, 4 * k4:4 * k4 + 4, :]
                            pv = banks[bo][0:65, :].rearrange("p (r i) -> p r i", r=4)
                        if g == 0:
                            S.op("act", I("activation", out=av, in_=pv, func=AF.Copy), reads=["bank%d" % bo], writes=["acc%d" % h2])
                        else:
                            S.op("dve", I("tensor_tensor", out=av, in0=pv, in1=av, op=ALU.add), reads=["bank%d" % bo, "acc%d" % h2], writes=["acc%d" % h2])

                LA = 2
                for idx in range(len(pairs) + LA):
                    if idx >= LA:
                        emit_pv(idx - LA)
                    if idx < len(pairs):
                        emit_qk(idx)
            norm_a(hp)
        norm_b(7)

        S.barrier()
        for k in range(8):
            for a in range(2):
                i = (2 * k + a) % 8
                DMA("sp", I("dma_start", out=stgB[i][:, :], in_=b_w_out[k][:, a * 512:(a + 1) * 512]), writes=["stgB%d" % i])
                if a:
                    S.op("act", I("activation", out=woutB[:, k, a * 512:(a + 1) * 512], in_=stgB[i][:, :], func=AF.Copy), reads=["stgB%d" % i], writes=["woutB"])
                else:
                    S.op("dve", I("tensor_copy", out=woutB[:, k, a * 512:(a + 1) * 512], in_=stgB[i][:, :]), reads=["stgB%d" % i], writes=["woutB"])
        NTB = OWN // TA

        for k in range(8):
            DMA("sp", I("dma_start", out=xall[:, k, :], in_=x1f[k]), reads=["x1f"], writes=["xall%d" % k])
            S.op("act", I("activation", out=xall[:, k, :], in_=xall[:, k, :], func=AF.Copy, scale=ALPHA), reads=["xall%d" % k], writes=["xall%d" % k])

        def stage_OB(ti):
            p = ti % 2
            p3 = ti % 3
            cs = slice(ti * TA, (ti + 1) * TA)
            for m in range(8):
                bo = nbank(0, 6)
                for c in range(8):
                    mm(banks[bo][:, 0:TA], woutB[:, c, m * 128:(m + 1) * 128], gT[:, c, cs], c == 0, c == 7, ["woutB", "gT"], ["bank%d" % bo])
                S.op("dve", I("scalar_tensor_tensor", out=rB[p3][:, m, :], in0=banks[bo][:, 0:TA], scalar=col(C_BBOUT, m), in1=xall[:, m, cs], op0=ALU.add, op1=ALU.add),
                     reads=["bank%d" % bo, "xall%d" % m, "cvec"], writes=["rB%d_%d" % (p3, m // 2)])
                if m % 2 == 1:
                    q_ = m // 2
                    S.op("dve", I("tensor_copy", out=rBb[p][:, m - 1:m + 1, :], in_=rB[p3][:, m - 1:m + 1, :]), reads=["rB%d_%d" % (p3, q_)], writes=["rBb%d" % p])
                    S.op("act", I("activation", out=rBq[p][:, m - 1:m + 1, :], in_=rB[p3][:, m - 1:m + 1, :], func=AF.Square), reads=["rB%d_%d" % (p3, q_)], writes=["rBq%d" % p])
                yield

        def stage_SB(ti):
            p = ti % 2
            p3 = ti % 3
            cs = slice(ti * TA, (ti + 1) * TA)
            yield
            yield
            yield from ln_stats(rBb[p], rBq[p], TA, "rBb%d" % p, "rBq%d" % p, st3, "s3", 6, 7)
            mean_sb, rstd_sb = st3
            yield

            def affine(c2):
                tv = ttB[c2 % 4].rearrange("p (k n) -> p k n", k=2)
                for j in range(2):
                    m = 2 * c2 + j
                    S.op("act", I("activation", out=rB[p3][:, m, :], in_=tv[:, j, :], func=AF.Identity, bias=col(C_PB1, m), scale=col(C_PG1, m)),
                         reads=["ttB%d" % (c2 % 4), "cvec"], writes=["rB%d_%d" % (p3, c2)])

            for c2 in range(4):
                t = ttB[c2 % 4]
                tv = t.rearrange("p (k n) -> p k n", k=2)
                S.op("dve", I("tensor_tensor", out=tv, in0=rB[p3][:, 2 * c2:2 * c2 + 2, :], in1=mean_sb.rearrange("p (k n) -> p k n", k=2), op=ALU.subtract),
                     reads=["rB%d_%d" % (p3, c2), "s3mean"], writes=["ttB%d" % (c2 % 4)])
                S.op("dve", I("tensor_tensor", out=t[:], in0=t[:], in1=rstd_sb[:], op=ALU.mult), reads=["ttB%d" % (c2 % 4), "s3rstd"], writes=["ttB%d" % (c2 % 4)])
                if c2 >= 1:
                    affine(c2 - 1)
                yield
            affine(3)
            DMA("sp", I("dma_start", out=outT[:, :, cs].rearrange("k p n -> p k n"), in_=rB[p3][:, :, :]), reads=["rB%d_%d" % (p3, q_) for q_ in range(4)], writes=["outT"])

        run(stage_OB(0))
        for ti in range(NTB):
            g = [stage_SB(ti)]
            if ti + 1 < NTB:
                g.insert(0, stage_OB(ti + 1))
            run(*g)
        S.barrier()

        with nc.Block() as block:
            @block.sync
            def _(e):
                S.replay("sp", e)

            @block.scalar
            def _(e):
                S.replay("act", e)

            @block.vector
            def _(e):
                S.replay("dve", e)

            @block.gpsimd
            def _(e):
                S.replay("pool", e)

            @block.tensor
            def _(e):
                S.replay("pe", e)
    return nc


def _host_inputs(x, a_w_in, a_b_in, a_w_dw, a_b_dw, a_ln_g, a_ln_b, a_w_out, a_b_out,
                 kv_w, b_w_in, b_w_out, b_b_out, post_ln_g, post_ln_b):
    f = np.float32
    x2 = np.asarray(x, f)[0]
    xpad = np.concatenate([np.zeros((PRE + HALO, D), f), x2], axis=0)

    def pcol(v):
        v = np.asarray(v, f).reshape(-1, 128)
        return v.T

    cvec = np.concatenate([pcol(a_b_in[0]), pcol(a_b_dw[0]), pcol(a_ln_g[0]), pcol(a_ln_b[0]), pcol(a_b_out[0]),
                           pcol(post_ln_g[0]), pcol(post_ln_b[0]), pcol(b_b_out[0]), pcol(post_ln_g[1]), pcol(post_ln_b[1])], axis=1)
    assert cvec.shape == (128, 96)
    wdw = np.asarray(a_w_dw, f)[0].reshape(31, 8, 128).transpose(2, 1, 0).reshape(128, 8 * 31)
    j = np.arange(128)[:, None]
    qi = np.arange(128)[None, :]
    prev = np.where(j >= qi, (qi + 128 - j).astype(f), f(BIG))
    cur = np.where(j <= qi, (qi - j).astype(f), f(BIG))
    distn = np.concatenate([prev, cur], axis=1).astype(f)
    common = {
        "a_w_in": np.ascontiguousarray(np.asarray(a_w_in, f)[0].reshape(8, 128, 3072)),
        "a_w_out": np.ascontiguousarray(np.asarray(a_w_out, f)[0].reshape(8, 128, 1024)),
        "kv_w": np.ascontiguousarray(np.asarray(kv_w, f).reshape(8, 128, 6144)),
        "b_w_in": np.ascontiguousarray(np.asarray(b_w_in, f)[0].reshape(8, 128, 4096)),
        "b_w_out": np.ascontiguousarray(np.asarray(b_w_out, f)[0].reshape(8, 128, 1024)),
        "cvec": np.ascontiguousarray(cvec),
        "wdw": np.ascontiguousarray(wdw),
        "ident": np.eye(128, dtype=f),
    }
    bigp = np.full((128, 128), BIG, f)
    first0 = np.concatenate([bigp, cur], axis=1).astype(f)
    in_maps = []
    for c in range(NCORES):
        xs = xpad[OWN * c:OWN * c + XCOLS]
        m = dict(common)
        m["xT"] = np.ascontiguousarray(xs.T.reshape(8, 128, XCOLS))
        hmv = np.zeros((128, 2), f)
        hmv[:, 0] = 0.0 if c <= 1 else 1.0
        hmv[:, 1] = 0.0 if c == 0 else 1.0
        m["hm"] = hmv
        fu = first0 if c == 0 else distn
        m["dist3"] = np.ascontiguousarray(np.concatenate([fu, distn], axis=1))
        in_maps.append(m)
    return in_maps


_NC = {}


def kernel(**inputs):
    in_maps = _host_inputs(**inputs)
    if "nc" not in _NC:
        _NC["nc"] = build_nc()
    res = run_bass_kernel_spmd(_NC["nc"], in_maps, core_ids=list(range(NCORES)))
    outs = [np.asarray(r["outT"], np.float32).reshape(D, OWN).T for r in res.results]
    if DEBUG:
        kernel.dbg = [np.asarray(r["x1f"], np.float32).reshape(D, OWN).T for r in res.results]
    return np.concatenate(outs, axis=0)[None].astype(np.float32)
```

```python
import numpy as np
from contextlib import ExitStack
import concourse.bass as bass
import concourse.mybir as mybir
from concourse.bass_utils import run_bass_kernel_spmd

F32 = mybir.dt.float32
BF16 = mybir.dt.bfloat16
AF = mybir.ActivationFunctionType
ALU = mybir.AluOpType

NCORES = 8
SEQ = 16384
D = 1024
OWN = 2048
HALO = 2048
PRE = 32
TA = 256
NTA = (HALO + OWN) // TA
XCOLS = PRE + HALO + OWN
ALPHA = 4.0 ** 0.25
LN_EPS = 1e-5
GROUPS = ((128, 1), (512, 4), (2048, 16))
BIG = 1.0e5
ENGS = ("pe", "act", "dve", "pool", "sp")
DEBUG = False


class Sched:
    def __init__(self, nc, sems):
        self.nc = nc
        self.sem = sems
        self.q = {e: [] for e in ENGS}
        self.cnt = {e: 0 for e in ENGS}
        self.seen = {e: {} for e in ENGS}
        self.last_w = {}
        self.readers = {}
        self.dma_sems = {}

    def _need(self, eng, tok, waits):
        if tok is None:
            return
        if tok[0] == "eng":
            _, pe, seq = tok
            if pe == eng and eng == "pe":
                return
            key = ("eng", pe)
        else:
            _, sn, seq = tok
            key = ("dma", sn)
        if self.seen[eng].get(key, 0) >= seq:
            return
        if waits.get(key, 0) < seq:
            waits[key] = seq

    def _deps(self, eng, reads, writes):
        waits = {}
        for r in reads:
            self._need(eng, self.last_w.get(r), waits)
        for w in writes:
            self._need(eng, self.last_w.get(w), waits)
            for t in self.readers.get(w, ()):
                if t[0] == "eng" and t[1] == eng and eng == "pe":
                    continue
                self._need(eng, t, waits)
        for key, v in waits.items():
            self.seen[eng][key] = v
        return list(waits.items())

    def op(self, eng, fn, reads=(), writes=(), mark=True):
        waits = self._deps(eng, reads, writes)
        if mark:
            self.cnt[eng] += 1
            tok = ("eng", eng, self.cnt[eng])
        else:
            tok = ("eng", eng, self.cnt[eng] + 1)
        self.q[eng].append(("op", fn, waits, mark))
        for w in writes:
            self.last_w[w] = tok
            self.readers[w] = []
        for r in reads:
            self.readers.setdefault(r, []).append(tok)
        return tok

    def dma(self, eng, semname, fn, reads=(), writes=()):
        waits = self._deps(eng, reads, writes)
        ent = self.dma_sems[semname]
        ent[1] += 16
        tok = ("dma", semname, ent[1])
        self.q[eng].append(("dma", fn, waits, semname))
        for w in writes:
            self.last_w[w] = tok
            self.readers[w] = []
        for r in reads:
            self.readers.setdefault(r, []).append(tok)
        return tok

    def barrier(self):
        for e in ENGS:
            waits = {}
            for p in ENGS:
                if p != e and self.cnt[p] > self.seen[e].get(("eng", p), 0):
                    waits[("eng", p)] = self.cnt[p]
            for sn, ent in self.dma_sems.items():
                if ent[1] > self.seen[e].get(("dma", sn), 0):
                    waits[("dma", sn)] = ent[1]
            for k, v in waits.items():
                self.seen[e][k] = v
            self.q[e].append(("wait", None, list(waits.items()), None))

    def replay(self, eng, e):
        for kind, fn, waits, extra in self.q[eng]:
            for key, v in waits:
                if key[0] == "eng":
                    e.wait_ge(self.sem[key[1]], v)
                else:
                    e.wait_ge(self.dma_sems[key[1]][0], v)
            if kind == "op":
                ins = fn(e)
                if extra:
                    ins.then_inc(self.sem[eng], 1)
            elif kind == "dma":
                fn(e).then_inc(self.dma_sems[extra][0], 16)


def I(name, **kw):
    return lambda e: getattr(e, name)(**kw)


def build_nc():
    nc = bass.Bass("TRN2", target_bir_lowering=False)

    def din(name, shape):
        return nc.dram_tensor(name, list(shape), F32, kind="ExternalInput").ap()

    xT = din("xT", [8, 128, XCOLS])
    a_w_in = din("a_w_in", [8, 128, 3072])
    a_w_out = din("a_w_out", [8, 128, 1024])
    kv_w = din("kv_w", [8, 128, 6144])
    b_w_in = din("b_w_in", [8, 128, 4096])
    b_w_out = din("b_w_out", [8, 128, 1024])
    cvec = din("cvec", [128, 96])
    wdw = din("wdw", [128, 8 * 31])
    hm = din("hm", [128, 2])
    ident = din("ident", [128, 128])
    dist3 = din("dist3", [128, 512])
    outT = nc.dram_tensor("outT", [8, 128, OWN], F32, kind="ExternalOutput").ap()
    x1f = nc.dram_tensor("x1f", [8, 128, OWN], F32, kind="ExternalOutput" if DEBUG else "Internal").ap()
    x1T_d = nc.dram_tensor("x1T_d", [8, 128, HALO + OWN], BF16, kind="Internal").ap()

    es = ExitStack()
    with es:
        def sb(name, shape, dt):
            return es.enter_context(nc.sbuf_tensor(name, list(shape), dt))

        NEL = 103936
        arena = sb("arena", [128, NEL], BF16)

        def AR(off, nbytes, dt=BF16):
            assert off % 4 == 0 and (off + nbytes) <= NEL * 2, (off, nbytes)
            ap = arena[:, off // 2:(off + nbytes) // 2]
            return ap.bitcast(F32) if dt == F32 else ap

        def K8(ap):
            return ap.rearrange("p (k n) -> p k n", k=8)

        cv_t = sb("cvec_t", [128, 96], F32)
        wdw_t = sb("wdw_t", [128, 8 * 31], F32)
        hm_t = sb("hm_t", [128, 2], F32)
        ident_t = sb("ident_t", [128, 128], F32)
        dist3_t = sb("dist3_t", [128, 512], F32)
        onesm = sb("onesm", [128, 128], BF16)
        onesf = sb("onesf", [128, 64], F32)
        banks = [es.enter_context(nc.psum_tensor("bank%d" % i, [128, 512], F32)) for i in range(8)]

        sems = {e: es.enter_context(nc.semaphore("s_" + e)) for e in ENGS}
        S = Sched(nc, sems)
        NDS = {"sp": 40, "pool": 24}
        for q_, n_ in NDS.items():
            for i in range(n_):
                S.dma_sems["%s%d" % (q_, i)] = [es.enter_context(nc.semaphore("d%s%d" % (q_, i))), 0]
        dsi = {"sp": 0, "pool": 0}

        def DMA(eng, fn, reads=(), writes=()):
            sn = "%s%d" % (eng, dsi[eng] % NDS[eng])
            dsi[eng] += 1
            prev = S.dma_sems[sn][1]
            if prev > S.seen[eng].get(("dma", sn), 0):
                S.q[eng].append(("wait", None, [(("dma", sn), prev)], None))
                S.seen[eng][("dma", sn)] = prev
            return S.dma(eng, sn, fn, reads=reads, writes=writes)

        rr = {}

        def nbank(lo, hi):
            k = (lo, hi)
            i = lo + rr.get(k, 0) % (hi - lo)
            rr[k] = rr.get(k, 0) + 1
            return i

        C_BIN, C_BDW, C_LNG, C_LNB, C_BOUT, C_PG0, C_PB0, C_BBOUT, C_PG1, C_PB1 = 0, 24, 32, 40, 48, 56, 64, 72, 80, 88

        def col(c0, i):
            return cv_t[:, c0 + i:c0 + i + 1]

        def mm(out, lhsT, rhs, start, stop, reads, writes):
            S.op("pe", I("matmul", out=out, lhsT=lhsT, rhs=rhs, start=start, stop=stop), reads=reads, writes=writes, mark=stop)

        for t, src, nm in ((cv_t, cvec, "cvec"), (wdw_t, wdw, "wdw"), (hm_t, hm, "hm"), (ident_t, ident, "ident"), (dist3_t, dist3, "dist3")):
            DMA("sp", I("dma_start", out=t[:], in_=src), writes=[nm])
        S.op("dve", I("memset", ap=onesm[:], constant=1.0 / 1024.0), writes=["onesm"])
        S.op("dve", I("memset", ap=onesf[:], constant=1.0), writes=["onesf"])

        o = 0
        NDT = 4
        NPT = 31 - NDT
        diag = AR(o, NPT * 8 * 256).rearrange("p (j c m) -> p j c m", j=NPT, c=8); o += NPT * 8 * 256
        winb = K8(AR(o, 8 * 3072 * 2)); o += 8 * 3072 * 2
        woutb = K8(AR(o, 8 * 1024 * 2)); o += 8 * 1024 * 2
        xt = K8(AR(o, 8 * TA * 4, F32)); o += 8 * TA * 4
        xb = [K8(AR(o + i * 8 * TA * 2, 8 * TA * 2)) for i in range(2)]; o += 2 * 8 * TA * 2
        UW = PRE + TA
        ub = [K8(AR(o + i * 8 * UW * 2, 8 * UW * 2)) for i in range(2)]; o += 2 * 8 * UW * 2
        sg = [AR(o + i * TA * 4, TA * 4, F32) for i in range(2)]; o += 2 * TA * 4
        szb = [K8(AR(o + i * 8 * TA * 2, 8 * TA * 2)) for i in range(2)]; o += 2 * 8 * TA * 2
        AR_cvf0, AR_cvf1 = AR(o, 2048, F32), AR(o + 4096, 2048, F32)
        cvf = K8(AR(o, 8 * TA * 4, F32)); o += 8 * TA * 4
        cvb = K8(AR(o, 8 * TA * 2)); o += 8 * TA * 2
        sqb = K8(AR(o, 8 * TA * 2)); o += 8 * TA * 2
        rbb, rsq = cvb, sqb
        st1 = [AR(o + i * TA * 8, TA * 8, F32) for i in range(2)]; o += 2 * TA * 8
        st2 = [AR(o + i * TA * 8, TA * 8, F32) for i in range(2)]; o += 2 * TA * 8
        tt = [AR(o + i * TA * 8, TA * 8, F32) for i in range(4)]; o += 4 * TA * 8
        yy = [AR(o + i * TA * 4, TA * 4).rearrange("p (k n) -> p k n", k=2) for i in range(4)]; o += 4 * TA * 4
        vbs = [K8(AR(o + i * 8 * TA * 2, 8 * TA * 2)) for i in range(2)]; o += 2 * 8 * TA * 2
        rf = xt
        assert o <= NEL * 2, o

        def ln_stats(srcb, sqsrc, n, r_src, r_sq, bufs, tag, bm, be):
            mean_sb, rstd_sb = bufs
            for c in range(8):
                mm(banks[bm][:, 0:n], onesm[:], srcb[:, c, 0:n], c == 0, c == 7, ["onesm", r_src], ["bank%d" % bm])
            for c in range(8):
                mm(banks[be][:, 0:n], onesm[:], sqsrc[:, c, 0:n], c == 0, c == 7, ["onesm", r_sq], ["bank%d" % be])
            n2 = 2 * n
            S.op("act", I("activation", out=rstd_sb[:, 0:n], in_=banks[bm][:, 0:n], func=AF.Square), reads=["bank%d" % bm], writes=[tag + "rstd"])
            S.op("act", I("activation", out=mean_sb[:, 0:n], in_=banks[bm][:, 0:n], func=AF.Copy), reads=["bank%d" % bm], writes=[tag + "mean"])
            S.op("act", I("activation", out=mean_sb[:, n:n2], in_=banks[bm][:, 0:n], func=AF.Copy), reads=["bank%d" % bm], writes=[tag + "mean"])
            yield
            S.op("dve", I("scalar_tensor_tensor", out=rstd_sb[:, 0:n], in0=rstd_sb[:, 0:n], scalar=-1.0, in1=banks[be][:, 0:n], op0=ALU.mult, op1=ALU.add),
                 reads=["bank%d" % be, tag + "rstd"], writes=[tag + "rstd"])
            S.op("act", I("activation", out=rstd_sb[:, 0:n], in_=rstd_sb[:, 0:n], func=AF.Sqrt, bias=LN_EPS, scale=1.0), reads=[tag + "rstd"], writes=[tag + "rstd"])
            yield
            S.op("dve", I("reciprocal", out=rstd_sb[:, 0:n], in_=rstd_sb[:, 0:n]), reads=[tag + "rstd"], writes=[tag + "rstd"])
            S.op("dve", I("tensor_copy", out=rstd_sb[:, n:n2], in_=rstd_sb[:, 0:n]), reads=[tag + "rstd"], writes=[tag + "rstd"])

        def build_diag(c):
            for jj in range(NPT):
                j = NDT + jj
                sel = 1 if jj % 3 == 2 else 0
                res = "diag%d_%d" % (c, sel)
                if sel == 0:
                    S.op("dve", I("tensor_scalar", out=diag[:, jj, c, :], in0=ident_t[:], scalar1=wdw_t[:, c * 31 + j:c * 31 + j + 1], scalar2=None, op0=ALU.mult),
                         reads=["ident", "wdw"], writes=[res])
                else:
                    S.op("act", I("activation", out=diag[:, jj, c, :], in_=ident_t[:], func=AF.Copy, scale=wdw_t[:, c * 31 + j:c * 31 + j + 1]),
                         reads=["ident", "wdw"], writes=[res])

        def diag_gen():
            for c in range(4, 8):
                yield
                build_diag(c)
                yield

        stg = [(AR_cvf0, ["cvf0", "cvf1"]), (AR_cvf1, ["cvf4", "cvf5"]), (vbs[0].rearrange("p k n -> p (k n)").bitcast(F32), ["vb0"]),
               (vbs[1].rearrange("p k n -> p (k n)").bitcast(F32), ["vb1"]), (st1[0], ["s1mean"]), (st1[1], ["s1rstd"]),
               (st2[0], ["s2mean"]), (st2[1], ["s2rstd"]), (tt[0], ["tt0"]), (tt[1], ["tt1"])]
        stg_i = {"i": 0}

        def load_piece(dst3, src3, dst_res, n_k):
            w = dst3.shape[2]
            per = max(1, 512 // w)
            for k0 in range(0, n_k, per):
                kk = min(per, n_k - k0)
                i = stg_i["i"] % len(stg)
                stg_i["i"] += 1
                buf, names = stg[i]
                bv = buf[:, 0:kk * w].rearrange("p (k n) -> p k n", k=kk)
                DMA("sp", I("dma_start", out=bv, in_=src3[:, k0:k0 + kk, :]), writes=["stg%d" % i])
                if i % 2:
                    S.op("act", I("activation", out=dst3[:, k0:k0 + kk, :], in_=bv, func=AF.Copy), reads=["stg%d" % i] + names, writes=[dst_res])
                else:
                    S.op("dve", I("tensor_copy", out=dst3[:, k0:k0 + kk, :], in_=bv), reads=["stg%d" % i] + names, writes=[dst_res])

        def load_win_piece(part, c):
            cs = slice(part * 1024 + c * 128, part * 1024 + (c + 1) * 128)
            load_piece(winb[:, :, cs], a_w_in[:, :, cs].rearrange("k p n -> p k n"), "winb%d_%d" % (part, c), 8)

        def load_x(ti):
            if ti < 0:
                DMA("pool", I("dma_start", out=xb[1][:, :, 0:PRE], in_=xT[:, :, 0:PRE].rearrange("k p n -> p k n")), writes=["xb1"])
            else:
                c0 = PRE + ti * TA
                DMA("pool", I("dma_start", out=xb[ti % 2][:, :, :], in_=xT[:, :, c0:c0 + TA].rearrange("k p n -> p k n")), writes=["xb%d" % (ti % 2)])

        load_x(-1)
        load_x(0)

        def late_weights():
            for c in range(8):
                load_win_piece(2, c)
                yield
            for k in range(8):
                load_piece(woutb[:, k, :].rearrange("p (a n) -> p a n", a=2), a_w_out[k].rearrange("p (a n) -> p a n", a=2), "woutb", 2)
                yield

        diag_done = {"c": 0}

        def glu_chunk(c, n, xsrc, xres_, udst, ures, ucol0):
            bg = nbank(0, 6)
            for k in range(8):
                mm(banks[bg][:, 0:n], winb[:, k, 1024 + c * 128:1024 + (c + 1) * 128], xsrc[:, k, 0:n], k == 0, k == 7, ["winb1_%d" % c, xres_], ["bank%d" % bg])
            s = sg[c % 2]
            S.op("act", I("activation", out=s[:, 0:n], in_=banks[bg][:, 0:n], func=AF.Sigmoid, bias=col(C_BIN, 8 + c), scale=1.0),
                 reads=["bank%d" % bg, "cvec"], writes=["sg%d" % (c % 2)])
            ba = nbank(0, 6)
            for k in range(8):
                mm(banks[ba][:, 0:n], winb[:, k, c * 128:(c + 1) * 128], xsrc[:, k, 0:n], k == 0, k == 7, ["winb0_%d" % c, xres_], ["bank%d" % ba])
            S.op("dve", I("scalar_tensor_tensor", out=udst[:, c, ucol0:ucol0 + n], in0=banks[ba][:, 0:n], scalar=col(C_BIN, c), in1=s[:, 0:n], op0=ALU.add, op1=ALU.mult),
                 reads=["bank%d" % ba, "sg%d" % (c % 2), "cvec"], writes=[ures])

        for c in range(8):
            for part in (1, 0):
                load_win_piece(part, c)
            glu_chunk(c, PRE, xb[1], "xb1", ub[0], "u0", 0)
            if c % 2 == 1:
                build_diag(c // 2)
        S.op("pool", I("tensor_scalar", out=ub[0][:, :, 0:PRE], in0=ub[0][:, :, 0:PRE], scalar1=hm_t[:, 0:1], scalar2=None, op0=ALU.mult),
             reads=["u0", "hm"], writes=["u0"])

        def stage_Ag(ti):
            p = ti % 2
            U, Un = ub[p], ub[1 - p]
            X = xb[p]
            if ti + 1 < NTA:
                load_x(ti + 1)
            for c in range(8):
                glu_chunk(c, TA, X, "xb%d" % p, U, "u%d" % p, PRE)
                yield
            if ti + 1 < NTA:
                if ti + 1 == HALO // TA:
                    S.op("pool", I("tensor_scalar", out=Un[:, :, 0:PRE], in0=U[:, :, TA:TA + PRE], scalar1=hm_t[:, 1:2], scalar2=None, op0=ALU.mult),
                         reads=["u%d" % p, "hm"], writes=["u%d" % (1 - p)])
                else:
                    S.op("pool", I("tensor_copy", out=Un[:, :, 0:PRE], in_=U[:, :, TA:TA + PRE]), reads=["u%d" % p], writes=["u%d" % (1 - p)])

        def stage_Az(ti):
            p = ti % 2
            X = xb[p]
            for c in range(8):
                bz = nbank(0, 6)
                for k in range(8):
                    mm(banks[bz][:, 0:TA], winb[:, k, 2048 + c * 128:2048 + (c + 1) * 128], X[:, k, :], k == 0, k == 7, ["winb2_%d" % c, "xb%d" % p], ["bank%d" % bz])
                S.op("act", I("activation", out=szb[p][:, c, :], in_=banks[bz][:, 0:TA], func=AF.Silu, bias=col(C_BIN, 16 + c), scale=1.0),
                     reads=["bank%d" % bz, "cvec"], writes=["szb%d" % p])
                yield

        def stage_C(ti):
            p = ti % 2
            U = ub[p]
            for c in range(8):
                bc = nbank(0, 6)
                for jj in range(NPT):
                    j = NDT + jj
                    mm(banks[bc][:, 0:TA], diag[:, jj, c, :], U[:, c, 2 + j:2 + j + TA], jj == 0, jj == NPT - 1,
                       ["diag%d_%d" % (c, 1 if jj % 3 == 2 else 0), "u%d" % p], ["bank%d" % bc])
                S.op("act", I("activation", out=cvf[:, c, :], in_=banks[bc][:, 0:TA], func=AF.Identity, bias=col(C_BDW, c), scale=1.0),
                     reads=["bank%d" % bc, "cvec"], writes=["cvf%d" % c])
                for j in range(NDT):
                    S.op("dve", I("scalar_tensor_tensor", out=cvf[:, c, :], in0=U[:, c, 2 + j:2 + j + TA], scalar=wdw_t[:, c * 31 + j:c * 31 + j + 1], in1=cvf[:, c, :],
                                  op0=ALU.mult, op1=ALU.add),
                         reads=["u%d" % p, "wdw", "cvf%d" % c], writes=["cvf%d" % c])
            allc = ["cvf%d" % c for c in range(8)]
            S.op("act", I("activation", out=sqb[:, :, :], in_=cvf[:, :, :], func=AF.Square), reads=allc, writes=["sqb"])
            S.op("dve", I("tensor_copy", out=cvb[:, :, :], in_=cvf[:, :, :]), reads=allc, writes=["cvb"])

        def stage_S1(ti):
            p = ti % 2
            yield
            yield
            yield from ln_stats(cvb, sqb, TA, "cvb", "sqb", st1, "s1", 6, 7)
            mean_sb, rstd_sb = st1
            yield

            def gate(c2):
                y = yy[c2 % 4]
                S.op("dve", I("tensor_tensor", out=vbs[p][:, 2 * c2:2 * c2 + 2, :], in0=y[:, :, :], in1=szb[p][:, 2 * c2:2 * c2 + 2, :], op=ALU.mult),
                     reads=["yy%d" % (c2 % 4), "szb%d" % p], writes=["vb%d" % p])

            for c2 in range(4):
                t, y = tt[c2 % 4], yy[c2 % 4]
                tv = t.rearrange("p (k n) -> p k n", k=2)
                S.op("dve", I("tensor_tensor", out=tv, in0=cvf[:, 2 * c2:2 * c2 + 2, :], in1=mean_sb.rearrange("p (k n) -> p k n", k=2), op=ALU.subtract),
                     reads=["cvf%d" % (2 * c2), "cvf%d" % (2 * c2 + 1), "s1mean"], writes=["tt%d" % (c2 % 4)])
                S.op("dve", I("tensor_tensor", out=t[:], in0=t[:], in1=rstd_sb[:], op=ALU.mult), reads=["tt%d" % (c2 % 4), "s1rstd"], writes=["tt%d" % (c2 % 4)])
                for j in range(2):
                    c = 2 * c2 + j
                    S.op("act", I("activation", out=y[:, j, :], in_=tv[:, j, :], func=AF.Silu, bias=col(C_LNB, c), scale=col(C_LNG, c)),
                         reads=["tt%d" % (c2 % 4), "cvec"], writes=["yy%d" % (c2 % 4)])
                if c2 >= 1:
                    gate(c2 - 1)
                yield
            gate(3)

        RF = ["rf0", "rf1", "rf2", "rf3"]

        def stage_O(ti):
            c0 = PRE + ti * TA
            DMA("sp", I("dma_start", out=xt[:, :, :], in_=xT[:, :, c0:c0 + TA].rearrange("k p n -> p k n")), writes=RF)
            S.op("dve", I("tensor_scalar", out=xt[:, :, :], in0=xt[:, :, :], scalar1=ALPHA, scalar2=None, op0=ALU.mult), reads=RF, writes=RF)
            for m in range(8):
                bo = nbank(0, 6)
                for c in range(8):
                    mm(banks[bo][:, 0:TA], woutb[:, c, m * 128:(m + 1) * 128], vbs[ti % 2][:, c, :], c == 0, c == 7, ["woutb", "vb%d" % (ti % 2)], ["bank%d" % bo])
                S.op("dve", I("scalar_tensor_tensor", out=rf[:, m, :], in0=banks[bo][:, 0:TA], scalar=col(C_BOUT, m), in1=xt[:, m, :], op0=ALU.add, op1=ALU.add),
                     reads=["bank%d" % bo, "rf%d" % (m // 2), "cvec"], writes=["rf%d" % (m // 2)])
                yield
            S.op("dve", I("tensor_copy", out=rbb[:, :, :], in_=rf[:, :, :]), reads=RF, writes=["cvb"])
            S.op("act", I("activation", out=rsq[:, :, :], in_=rf[:, :, :], func=AF.Square), reads=RF, writes=["sqb"])

        def stage_S2(ti):
            own = ti >= HALO // TA
            yield from ln_stats(rbb, rsq, TA, "cvb", "sqb", st2, "s2", 6, 7)
            mean_sb, rstd_sb = st2
            yield
            def affine(c2):
                tv = tt[c2 % 4].rearrange("p (k n) -> p k n", k=2)
                for j in range(2):
                    m = 2 * c2 + j
                    dst = rf[:, m, :] if own else rbb[:, m, :]
                    S.op("act", I("activation", out=dst, in_=tv[:, j, :], func=AF.Identity, bias=col(C_PB0, m), scale=col(C_PG0, m)),
                         reads=["tt%d" % (c2 % 4), "cvec"], writes=["rf%d" % c2 if own else "cvb"])

            for c2 in range(4):
                t = tt[c2 % 4]
                tv = t.rearrange("p (k n) -> p k n", k=2)
                S.op("dve", I("tensor_tensor", out=tv, in0=rf[:, 2 * c2:2 * c2 + 2, :], in1=mean_sb.rearrange("p (k n) -> p k n", k=2), op=ALU.subtract),
                     reads=["rf%d" % c2, "s2mean"], writes=["tt%d" % (c2 % 4)])
                S.op("dve", I("tensor_tensor", out=t[:], in0=t[:], in1=rstd_sb[:], op=ALU.mult), reads=["tt%d" % (c2 % 4), "s2rstd"], writes=["tt%d" % (c2 % 4)])
                if c2 >= 1:
                    affine(c2 - 1)
                yield
            affine(3)
            yield
            pos0 = ti * TA
            if own:
                S.op("dve", I("tensor_copy", out=rbb[:, :, :], in_=rf[:, :, :]), reads=RF, writes=["cvb"])
                p1 = pos0 - HALO
                DMA("sp", I("dma_start", out=x1f[:, :, p1:p1 + TA].rearrange("k p n -> p k n"), in_=rf[:, :, :]), reads=RF, writes=["x1f"])
            DMA("sp", I("dma_start", out=x1T_d[:, :, pos0:pos0 + TA].rearrange("k p n -> p k n"), in_=rbb[:, :, :]), reads=["cvb"], writes=["x1T_d"])

        def run(*gens):
            gens = list(gens)
            while gens:
                for g in list(gens):
                    try:
                        next(g)
                    except StopIteration:
                        gens.remove(g)

        run(stage_Ag(0), late_weights(), diag_gen())
        run(stage_Az(0))
        for i in range(NTA + 1):
            if i < NTA:
                stage_C(i)
            g = []
            if i >= 1:
                g.append(stage_O(i - 1))
            if i < NTA:
                g.append(stage_S1(i))
            if i + 1 < NTA:
                g.append(stage_Az(i + 1))
            run(*g)
            g = []
            if i + 1 < NTA:
                g.append(stage_Ag(i + 1))
            if i >= 1:
                g.append(stage_S2(i - 1))
            run(*g)

        S.barrier()

        NT = HALO + OWN
        o = 0
        x1T = K8(AR(o, 8 * NT * 2)); o += 8 * NT * 2
        gT = K8(AR(o, 8 * OWN * 2)); o += 8 * OWN * 2
        acc = [AR(o + i * OWN * 4, OWN * 4, F32) for i in range(2)]; o += 2 * OWN * 4
        szhs = [AR(o + i * OWN * 4, OWN * 4, F32) for i in range(2)]; o += 2 * OWN * 4
        o_attn = o
        wsl = [AR(o + i * 20480, 20480).rearrange("p (s k n) -> p s k n", s=10, k=8) for i in range(2)]; o += 2 * 20480
        qd = AR(o, OWN * 2); o += OWN * 2
        kd = AR(o, NT * 2); o += NT * 2
        vaug = AR(o, 32 * 2 * 65 * 2).rearrange("p (b h e) -> p b h e", b=32, h=2); o += 32 * 2 * 65 * 2
        D2 = AR(o, 2 * 2 * 512 * 2).rearrange("p (h v n) -> p h v n", h=2, v=2); o += 2 * 2 * 512 * 2
        eb = [AR(o + i * 1024, 1024) for i in range(2)]; o += 2 * 1024
        pb = [AR(o + i * 1024, 1024) for i in range(3)]; o += 3 * 1024
        tnb = AR(o, 512 * 4, F32); o += 512 * 4
        omax = o
        o = o_attn
        woutB = K8(AR(o, 8 * 1024 * 2)); o += 8 * 1024 * 2
        xall = K8(AR(0, 8 * OWN * 4, F32))
        rB = [K8(AR(o + i * 8 * TA * 4, 8 * TA * 4, F32)) for i in range(3)]; o += 3 * 8 * TA * 4
        rBb = [K8(AR(o + i * 8 * TA * 2, 8 * TA * 2)) for i in range(2)]; o += 2 * 8 * TA * 2
        rBq = [K8(AR(o + i * 8 * TA * 2, 8 * TA * 2)) for i in range(2)]; o += 2 * 8 * TA * 2
        st3 = [AR(o + i * TA * 8, TA * 8, F32) for i in range(2)]; o += 2 * TA * 8
        o_dead = 8 * NT * 2 + 8 * OWN * 2
        ttB = [AR(o_dead + i * TA * 8, TA * 8, F32) for i in range(4)]
        stgB = [AR(o_dead + 4 * TA * 8 + i * 2048, 2048, F32) for i in range(8)]
        assert max(o, omax) <= NEL * 2, (o, omax)

        for k in range(8):
            pass
        def xblocks(a0, n, step=1):
            last = a0 + (n - 1) * step
            return ["x1T_b%d" % b for b in range(a0 // 1024, last // 1024 + 1)]
        for blk in (2, 3, 0, 1):
            cs_ = slice(blk * 1024, (blk + 1) * 1024)
            DMA("sp", I("dma_start", out=x1T[:, :, cs_], in_=x1T_d[:, :, cs_].rearrange("k p n -> p k n")), reads=["x1T_d"], writes=["x1T_b%d" % blk])
        S.op("dve", I("memset", ap=vaug[:, :, :, 64:65], constant=1.0), writes=["vaug"])

        slopes = [2.0 ** (-8.0 * (h + 1) / 16.0) for h in range(16)]

        def load_w(hp):
            W = wsl[hp % 2]
            srcs = []
            for g in range(3):
                srcs.append((b_w_in, g * 1024 + hp * 128))
            for g in range(3):
                srcs.append((kv_w, g * 1024 + hp * 128))
            for g in range(3):
                srcs.append((kv_w, 3072 + g * 1024 + hp * 128))
            srcs.append((b_w_in, 3072 + hp * 128))
            order = [9, 0, 3, 6, 1, 4, 7, 2, 5, 8]
            for s in order:
                wsrc, c0 = srcs[s]
                DMA("pool", I("dma_start", out=W[:, s, :, :], in_=wsrc[:, :, c0:c0 + 128].rearrange("k p n -> p k n")), writes=["wsl%d_%d" % (hp % 2, s)])

        def norm_a(hp):
            for h2 in range(2):
                A = acc[h2]
                S.op("act", I("activation", out=A[64:65, :], in_=A[64:65, :], func=AF.Ln), reads=["acc%d" % h2], writes=["acc%d" % h2])
            for h2 in range(2):
                A = acc[h2]
                S.op("act", I("activation", out=A[64:65, :], in_=A[64:65, :], func=AF.Exp, scale=-1.0), reads=["acc%d" % h2], writes=["acc%d" % h2])

        def norm_b(hp):
            szh = szhs[hp % 2]
            for h2 in range(2):
                A = acc[h2]
                pl = slice(64 * h2, 64 * h2 + 64)
                for t4 in range(OWN // 512):
                    cs = slice(t4 * 512, (t4 + 1) * 512)
                    bb = nbank(0, 3)
                    S.op("pe", I("matmul", out=banks[bb][0:64, :], lhsT=onesf[64:65, 0:64], rhs=A[64:65, cs], start=True, stop=True),
                         reads=["onesf", "acc%d" % h2], writes=["bank%d" % bb])
                    S.op("dve", I("tensor_tensor", out=tnb[pl, :], in0=A[0:64, cs], in1=banks[bb][0:64, :], op=ALU.mult), reads=["acc%d" % h2, "bank%d" % bb], writes=["tnb"])
                    S.op("dve", I("tensor_tensor", out=gT[pl, hp, cs], in0=tnb[pl, :], in1=szh[pl, cs], op=ALU.mult), reads=["tnb", "szh%d" % (hp % 2)], writes=["gT"])

        load_w(0)
        for hp in range(8):
            W = wsl[hp % 2]
            wr = lambda s: "wsl%d_%d" % (hp % 2, s)
            for t4 in range(OWN // 512):
                bz = nbank(0, 3)
                for k in range(8):
                    mm(banks[bz][:, :], W[:, 9, k, :], x1T[:, k, HALO + t4 * 512:HALO + (t4 + 1) * 512], k == 0, k == 7, [wr(9)] + xblocks(HALO + t4 * 512, 512), ["bank%d" % bz])
                S.op("act", I("activation", out=szhs[hp % 2][:, t4 * 512:(t4 + 1) * 512], in_=banks[bz][:, :], func=AF.Silu), reads=["bank%d" % bz], writes=["szh%d" % (hp % 2)])
            if hp + 1 < 8:
                load_w(hp + 1)
            for g, (window, d) in enumerate(GROUPS):
                halo = 128 * d
                nq = 16 // d
                qv = qd.rearrange("p (r n i) -> p r n i", r=d, n=nq)
                kv_ = kd[:, 0:d * (nq + 1) * 128].rearrange("p (r n i) -> p r n i", r=d, n=nq + 1)
                for t4 in range(OWN // 512):
                    bq = nbank(0, 3)
                    for k in range(8):
                        mm(banks[bq][:, :], W[:, g, k, :], x1T[:, k, HALO + t4 * 512:HALO + (t4 + 1) * 512], k == 0, k == 7, [wr(g)] + xblocks(HALO + t4 * 512, 512), ["bank%d" % bq])
                    L = 512 // d
                    l0 = t4 * L
                    nb_, i0 = l0 // 128, l0 % 128
                    if d == 1:
                        dst, src = qd[:, t4 * 512:(t4 + 1) * 512], banks[bq][:, :]
                    else:
                        dst, src = qv[:, :, nb_, i0:i0 + L], banks[bq][:, :].rearrange("p (l r) -> p r l", r=d)
                    S.op("act", I("activation", out=dst, in_=src, func=AF.Copy, scale=0.125), reads=["bank%d" % bq], writes=["qd"])
                ntok = halo + OWN
                for t0 in range(0, ntok, 512):
                    n = min(512, ntok - t0)
                    bk = nbank(0, 3)
                    a0 = HALO - halo + t0
                    for k in range(8):
                        mm(banks[bk][:, 0:n], W[:, 3 + g, k, :], x1T[:, k, a0:a0 + n], k == 0, k == 7, [wr(3 + g)] + xblocks(a0, n), ["bank%d" % bk])
                    L = n // d
                    l0 = t0 // d
                    nb_, i0 = l0 // 128, l0 % 128
                    if d == 1:
                        dst, src = kd[:, t0:t0 + n], banks[bk][:, 0:n]
                    else:
                        dst, src = kv_[:, :, nb_, i0:i0 + L], banks[bk][:, 0:n].rearrange("p (l r) -> p r l", r=d)
                    S.op("dve", I("tensor_copy", out=dst, in_=src), reads=["bank%d" % bk], writes=["kd"])
                nblk = d * (nq + 1)
                for blk in range(nblk):
                    r, b = blk // (nq + 1), blk % (nq + 1)
                    bv = nbank(0, 3)
                    a0 = HALO - halo + d * b * 128 + r
                    for k in range(8):
                        mm(banks[bv][:, 0:128], x1T[:, k, a0:a0 + 127 * d + 1:d], W[:, 6 + g, k, :], k == 0, k == 7, [wr(6 + g)] + xblocks(a0, 128, d), ["bank%d" % bv])
                    src = banks[bv][:, 0:128].rearrange("p (h e) -> p h e", h=2)
                    if blk % 3 == 2:
                        S.op("act", I("activation", out=vaug[:, blk, :, 0:64], in_=src, func=AF.Copy), reads=["bank%d" % bv], writes=["vaug"])
                    else:
                        S.op("dve", I("tensor_copy", out=vaug[:, blk, :, 0:64], in_=src), reads=["bank%d" % bv], writes=["vaug"])
                for h2 in range(2):
                    sc = -slopes[hp * 2 + h2] * d
                    df_, dn_ = dist3_t[:, 0:256], dist3_t[:, 256:512]
                    S.op("act", I("activation", out=D2[:, h2, 0, 0:256], in_=df_, func=AF.Exp, scale=sc), reads=["dist3"], writes=["D2"])
                    S.op("act", I("activation", out=D2[:, h2, 0, 256:512], in_=(df_ if d == 16 else dn_), func=AF.Exp, scale=sc), reads=["dist3"], writes=["D2"])
                    if d != 16:
                        S.op("act", I("activation", out=D2[:, h2, 1, 0:256], in_=dn_, func=AF.Exp, scale=sc), reads=["dist3"], writes=["D2"])
                        S.op("act", I("activation", out=D2[:, h2, 1, 256:512], in_=dn_, func=AF.Exp, scale=sc), reads=["dist3"], writes=["D2"])
                if g == 0 and hp >= 1:
                    norm_b(hp - 1)
                pairs = [(h2, P) for h2 in range(2) for P in range(8)]

                def kq(h2, u):
                    pl = slice(64 * h2, 64 * h2 + 64)
                    r, n = u // nq, u % nq
                    if d == 1:
                        return kd[pl, n * 128:(n + 1) * 128], kd[pl, (n + 1) * 128:(n + 2) * 128], qd[pl, n * 128:(n + 1) * 128], n
                    return kv_[pl, r, n, :], kv_[pl, r, n + 1, :], qv[pl, r, n, :], r * (nq + 1) + n

                def emit_qk(idx):
                    h2, P = pairs[idx]
                    bs = 3 + idx % 3
                    for j in range(2):
                        kp, kc, q, _ = kq(h2, 2 * P + j)
                        mm(banks[bs][:, j * 256:j * 256 + 128], kp, q, True, True, ["kd", "qd"], ["bank%d" % bs])
                        mm(banks[bs][:, j * 256 + 128:j * 256 + 256], kc, q, True, True, ["kd", "qd"], ["bank%d" % bs])
                    E, Pb = eb[idx % 2], pb[idx % 3]
                    S.op("act", I("activation", out=E[:, :], in_=banks[bs][:, :], func=AF.Exp), reads=["bank%d" % bs], writes=["eb%d" % (idx % 2)])
                    var = 0 if (d == 16 or (2 * P) % nq == 0) else 1
                    S.op("dve", I("tensor_tensor", out=Pb[:, :], in0=E[:, :], in1=D2[:, h2, var, :], op=ALU.mult), reads=["eb%d" % (idx % 2), "D2"], writes=["pb%d" % (idx % 3)])

                def emit_pv(idx):
                    h2, P = pairs[idx]
                    Pb = pb[idx % 3]
                    bo = 6 + (P // 2) % 2
                    A = acc[h2]
                    for j in range(2):
                        u = 2 * P + j
                        _, _, _, blk0 = kq(h2, u)
                        oc = (u % 4) * 128
                        mm(banks[bo][0:65, oc:oc + 128], vaug[:, blk0, h2, :], Pb[:, j * 256:j * 256 + 128], True, False, ["vaug", "pb%d" % (idx % 3)], ["bank%d" % bo])
                        mm(banks[bo][0:65, oc:oc + 128], vaug[:, blk0 + 1, h2, :], Pb[:, j * 256 + 128:j * 256 + 256], False, True, ["vaug", "pb%d" % (idx % 3)], ["bank%d" % bo])
                    if P % 2 == 1:
                        k4 = P // 2
                        if d == 1:
                            av, pv = A[0:65, k4 * 512:(k4 + 1) * 512], banks[bo][0:65, :]
                        elif d == 4:
                            av, pv = A[0:65, :].rearrange("p (l r) -> p r l", r=4)[:, k4, :], banks[bo][0:65, :]
                        else:
                            av = A[0:65, :].rearrange("p (i r) -> p r i", r=16)[:, 4 * k4:4 * k4 + 4, :]
                            pv = banks[bo][0:65, :].rearrange("p (r i) -> p r i", r=4)
                        if g == 0:
                            S.op("act", I("activation", out=av, in_=pv, func=AF.Copy), reads=["bank%d" % bo], writes=["acc%d" % h2])
                        else:
                            S.op("dve", I("tensor_tensor", out=av, in0=pv, in1=av, op=ALU.add), reads=["bank%d" % bo, "acc%d" % h2], writes=["acc%d" % h2])

                LA = 2
                for idx in range(len(pairs) + LA):
                    if idx >= LA:
                        emit_pv(idx - LA)
                    if idx < len(pairs):
                        emit_qk(idx)
            norm_a(hp)
        norm_b(7)

        S.barrier()
        for k in range(8):
            for a in range(2):
                i = (2 * k + a) % 8
                DMA("sp", I("dma_start", out=stgB[i][:, :], in_=b_w_out[k][:, a * 512:(a + 1) * 512]), writes=["stgB%d" % i])
                if a:
                    S.op("act", I("activation", out=woutB[:, k, a * 512:(a + 1) * 512], in_=stgB[i][:, :], func=AF.Copy), reads=["stgB%d" % i], writes=["woutB"])
                else:
                    S.op("dve", I("tensor_copy", out=woutB[:, k, a * 512:(a + 1) * 512], in_=stgB[i][:, :]), reads=["stgB%d" % i], writes=["woutB"])
        NTB = OWN // TA

        for k in range(8):
            DMA("sp", I("dma_start", out=xall[:, k, :], in_=x1f[k]), reads=["x1f"], writes=["xall%d" % k])
            S.op("act", I("activation", out=xall[:, k, :], in_=xall[:, k, :], func=AF.Copy, scale=ALPHA), reads=["xall%d" % k], writes=["xall%d" % k])

        def stage_OB(ti):
            p = ti % 2
            p3 = ti % 3
            cs = slice(ti * TA, (ti + 1) * TA)
            for m in range(8):
                bo = nbank(0, 6)
                for c in range(8):
                    mm(banks[bo][:, 0:TA], woutB[:, c, m * 128:(m + 1) * 128], gT[:, c, cs], c == 0, c == 7, ["woutB", "gT"], ["bank%d" % bo])
                S.op("dve", I("scalar_tensor_tensor", out=rB[p3][:, m, :], in0=banks[bo][:, 0:TA], scalar=col(C_BBOUT, m), in1=xall[:, m, cs], op0=ALU.add, op1=ALU.add),
                     reads=["bank%d" % bo, "xall%d" % m, "cvec"], writes=["rB%d_%d" % (p3, m // 2)])
                if m % 2 == 1:
                    q_ = m // 2
                    S.op("dve", I("tensor_copy", out=rBb[p][:, m - 1:m + 1, :], in_=rB[p3][:, m - 1:m + 1, :]), reads=["rB%d_%d" % (p3, q_)], writes=["rBb%d" % p])
                    S.op("act", I("activation", out=rBq[p][:, m - 1:m + 1, :], in_=rB[p3][:, m - 1:m + 1, :], func=AF.Square), reads=["rB%d_%d" % (p3, q_)], writes=["rBq%d" % p])
                yield

        def stage_SB(ti):
            p = ti % 2
            p3 = ti % 3
            cs = slice(ti * TA, (ti + 1) * TA)
            yield
            yield
            yield from ln_stats(rBb[p], rBq[p], TA, "rBb%d" % p, "rBq%d" % p, st3, "s3", 6, 7)
            mean_sb, rstd_sb = st3
            yield

            def affine(c2):
                tv = ttB[c2 % 4].rearrange("p (k n) -> p k n", k=2)
                for j in range(2):
                    m = 2 * c2 + j
                    S.op("act", I("activation", out=rB[p3][:, m, :], in_=tv[:, j, :], func=AF.Identity, bias=col(C_PB1, m), scale=col(C_PG1, m)),
                         reads=["ttB%d" % (c2 % 4), "cvec"], writes=["rB%d_%d" % (p3, c2)])

            for c2 in range(4):
                t = ttB[c2 % 4]
                tv = t.rearrange("p (k n) -> p k n", k=2)
                S.op("dve", I("tensor_tensor", out=tv, in0=rB[p3][:, 2 * c2:2 * c2 + 2, :], in1=mean_sb.rearrange("p (k n) -> p k n", k=2), op=ALU.subtract),
                     reads=["rB%d_%d" % (p3, c2), "s3mean"], writes=["ttB%d" % (c2 % 4)])
                S.op("dve", I("tensor_tensor", out=t[:], in0=t[:], in1=rstd_sb[:], op=ALU.mult), reads=["ttB%d" % (c2 % 4), "s3rstd"], writes=["ttB%d" % (c2 % 4)])
                if c2 >= 1:
                    affine(c2 - 1)
                yield
            affine(3)
            DMA("sp", I("dma_start", out=outT[:, :, cs].rearrange("k p n -> p k n"), in_=rB[p3][:, :, :]), reads=["rB%d_%d" % (p3, q_) for q_ in range(4)], writes=["outT"])

        run(stage_OB(0))
        for ti in range(NTB):
            g = [stage_SB(ti)]
            if ti + 1 < NTB:
                g.insert(0, stage_OB(ti + 1))
            run(*g)
        S.barrier()

        with nc.Block() as block:
            @block.sync
            def _(e):
                S.replay("sp", e)

            @block.scalar
            def _(e):
                S.replay("act", e)

            @block.vector
            def _(e):
                S.replay("dve", e)

            @block.gpsimd
            def _(e):
                S.replay("pool", e)

            @block.tensor
            def _(e):
                S.replay("pe", e)
    return nc


def _host_inputs(x, a_w_in, a_b_in, a_w_dw, a_b_dw, a_ln_g, a_ln_b, a_w_out, a_b_out,
                 kv_w, b_w_in, b_w_out, b_b_out, post_ln_g, post_ln_b):
    f = np.float32
    x2 = np.asarray(x, f)[0]
    xpad = np.concatenate([np.zeros((PRE + HALO, D), f), x2], axis=0)

    def pcol(v):
        v = np.asarray(v, f).reshape(-1, 128)
        return v.T

    cvec = np.concatenate([pcol(a_b_in[0]), pcol(a_b_dw[0]), pcol(a_ln_g[0]), pcol(a_ln_b[0]), pcol(a_b_out[0]),
                           pcol(post_ln_g[0]), pcol(post_ln_b[0]), pcol(b_b_out[0]), pcol(post_ln_g[1]), pcol(post_ln_b[1])], axis=1)
    assert cvec.shape == (128, 96)
    wdw = np.asarray(a_w_dw, f)[0].reshape(31, 8, 128).transpose(2, 1, 0).reshape(128, 8 * 31)
    j = np.arange(128)[:, None]
    qi = np.arange(128)[None, :]
    prev = np.where(j >= qi, (qi + 128 - j).astype(f), f(BIG))
    cur = np.where(j <= qi, (qi - j).astype(f), f(BIG))
    distn = np.concatenate([prev, cur], axis=1).astype(f)
    common = {
        "a_w_in": np.ascontiguousarray(np.asarray(a_w_in, f)[0].reshape(8, 128, 3072)),
        "a_w_out": np.ascontiguousarray(np.asarray(a_w_out, f)[0].reshape(8, 128, 1024)),
        "kv_w": np.ascontiguousarray(np.asarray(kv_w, f).reshape(8, 128, 6144)),
        "b_w_in": np.ascontiguousarray(np.asarray(b_w_in, f)[0].reshape(8, 128, 4096)),
        "b_w_out": np.ascontiguousarray(np.asarray(b_w_out, f)[0].reshape(8, 128, 1024)),
        "cvec": np.ascontiguousarray(cvec),
        "wdw": np.ascontiguousarray(wdw),
        "ident": np.eye(128, dtype=f),
    }
    bigp = np.full((128, 128), BIG, f)
    first0 = np.concatenate([bigp, cur], axis=1).astype(f)
    in_maps = []
    for c in range(NCORES):
        xs = xpad[OWN * c:OWN * c + XCOLS]
        m = dict(common)
        m["xT"] = np.ascontiguousarray(xs.T.reshape(8, 128, XCOLS))
        hmv = np.zeros((128, 2), f)
        hmv[:, 0] = 0.0 if c <= 1 else 1.0
        hmv[:, 1] = 0.0 if c == 0 else 1.0
        m["hm"] = hmv
        fu = first0 if c == 0 else distn
        m["dist3"] = np.ascontiguousarray(np.concatenate([fu, distn], axis=1))
        in_maps.append(m)
    return in_maps


_NC = {}


def kernel(**inputs):
    in_maps = _host_inputs(**inputs)
    if "nc" not in _NC:
        _NC["nc"] = build_nc()
    res = run_bass_kernel_spmd(_NC["nc"], in_maps, core_ids=list(range(NCORES)))
    outs = [np.asarray(r["outT"], np.float32).reshape(D, OWN).T for r in res.results]
    if DEBUG:
        kernel.dbg = [np.asarray(r["x1f"], np.float32).reshape(D, OWN).T for r in res.results]
    return np.concatenate(outs, axis=0)[None].astype(np.float32)
```

```python
import numpy as np
from contextlib import ExitStack
import concourse.bass as bass
import concourse.mybir as mybir
from concourse.bass_utils import run_bass_kernel_spmd

F32 = mybir.dt.float32
BF16 = mybir.dt.bfloat16
AF = mybir.ActivationFunctionType
ALU = mybir.AluOpType

NCORES = 8
SEQ = 16384
D = 1024
OWN = 2048
HALO = 2048
PRE = 32
TA = 256
NTA = (HALO + OWN) // TA
XCOLS = PRE + HALO + OWN
ALPHA = 4.0 ** 0.25
LN_EPS = 1e-5
GROUPS = ((128, 1), (512, 4), (2048, 16))
BIG = 1.0e5
ENGS = ("pe", "act", "dve", "pool", "sp")
DEBUG = False


class Sched:
    def __init__(self, nc, sems):
        self.nc = nc
        self.sem = sems
        self.q = {e: [] for e in ENGS}
        self.cnt = {e: 0 for e in ENGS}
        self.seen = {e: {} for e in ENGS}
        self.last_w = {}
        self.readers = {}
        self.dma_sems = {}

    def _need(self, eng, tok, waits):
        if tok is None:
            return
        if tok[0] == "eng":
            _, pe, seq = tok
            if pe == eng and eng == "pe":
                return
            key = ("eng", pe)
        else:
            _, sn, seq = tok
            key = ("dma", sn)
        if self.seen[eng].get(key, 0) >= seq:
            return
        if waits.get(key, 0) < seq:
            waits[key] = seq

    def _deps(self, eng, reads, writes):
        waits = {}
        for r in reads:
            self._need(eng, self.last_w.get(r), waits)
        for w in writes:
            self._need(eng, self.last_w.get(w), waits)
            for t in self.readers.get(w, ()):
                if t[0] == "eng" and t[1] == eng and eng == "pe":
                    continue
                self._need(eng, t, waits)
        for key, v in waits.items():
            self.seen[eng][key] = v
        return list(waits.items())

    def op(self, eng, fn, reads=(), writes=(), mark=True):
        waits = self._deps(eng, reads, writes)
        if mark:
            self.cnt[eng] += 1
            tok = ("eng", eng, self.cnt[eng])
        else:
            tok = ("eng", eng, self.cnt[eng] + 1)
        self.q[eng].append(("op", fn, waits, (tok[2] if mark else 0)))
        for w in writes:
            self.last_w[w] = tok
            self.readers[w] = []
        for r in reads:
            self.readers.setdefault(r, []).append(tok)
        return tok

    def dma(self, eng, semname, fn, reads=(), writes=()):
        waits = self._deps(eng, reads, writes)
        ent = self.dma_sems[semname]
        ent[1] += 16
        tok = ("dma", semname, ent[1])
        self.q[eng].append(("dma", fn, waits, semname))
        for w in writes:
            self.last_w[w] = tok
            self.readers[w] = []
        for r in reads:
            self.readers.setdefault(r, []).append(tok)
        return tok

    def barrier(self):
        for e in ENGS:
            waits = {}
            for p in ENGS:
                if p != e and self.cnt[p] > self.seen[e].get(("eng", p), 0):
                    waits[("eng", p)] = self.cnt[p]
            for sn, ent in self.dma_sems.items():
                if ent[1] > self.seen[e].get(("dma", sn), 0):
                    waits[("dma", sn)] = ent[1]
            for k, v in waits.items():
                self.seen[e][k] = v
            self.q[e].append(("wait", None, list(waits.items()), None))

    def _prepare(self):
        import bisect
        need = {e: set() for e in ENGS}
        for e in ENGS:
            for kind, fn, waits, extra in self.q[e]:
                for key, v in waits:
                    if key[0] == "eng":
                        need[key[1]].add(v)
        self._need_sorted = {e: sorted(need[e]) for e in ENGS}
        self._need_set = need
        self._rank = lambda p, v: bisect.bisect_right(self._need_sorted[p], v)

    def replay(self, eng, e):
        if not hasattr(self, "_need_set"):
            self._prepare()
        for kind, fn, waits, extra in self.q[eng]:
            for key, v in waits:
                if key[0] == "eng":
                    e.wait_ge(self.sem[key[1]], self._rank(key[1], v))
                else:
                    e.wait_ge(self.dma_sems[key[1]][0], v)
            if kind == "op":
                ins = fn(e)
                if extra and extra in self._need_set[eng]:
                    ins.then_inc(self.sem[eng], 1)
            elif kind == "dma":
                fn(e).then_inc(self.dma_sems[extra][0], 16)


def I(name, **kw):
    return lambda e: getattr(e, name)(**kw)


def build_nc():
    nc = bass.Bass("TRN2", target_bir_lowering=False)

    def din(name, shape):
        return nc.dram_tensor(name, list(shape), F32, kind="ExternalInput").ap()

    xT = din("xT", [8, 128, XCOLS])
    a_w_in = din("a_w_in", [8, 128, 3072])
    a_w_out = din("a_w_out", [8, 128, 1024])
    kv_w = din("kv_w", [8, 128, 6144])
    b_w_in = din("b_w_in", [8, 128, 4096])
    b_w_out = din("b_w_out", [8, 128, 1024])
    cvec = din("cvec", [128, 96])
    wdw = din("wdw", [128, 8 * 31])
    hm = din("hm", [128, 2])
    ident = din("ident", [128, 128])
    dist3 = din("dist3", [128, 512])
    outT = nc.dram_tensor("outT", [8, 128, OWN], F32, kind="ExternalOutput").ap()
    x1f = nc.dram_tensor("x1f", [8, 128, OWN], F32, kind="ExternalOutput" if DEBUG else "Internal").ap()
    x1T_d = nc.dram_tensor("x1T_d", [8, 128, HALO + OWN], BF16, kind="Internal").ap()

    es = ExitStack()
    with es:
        def sb(name, shape, dt):
            return es.enter_context(nc.sbuf_tensor(name, list(shape), dt))

        NEL = 103936
        arena = sb("arena", [128, NEL], BF16)

        def AR(off, nbytes, dt=BF16):
            assert off % 4 == 0 and (off + nbytes) <= NEL * 2, (off, nbytes)
            ap = arena[:, off // 2:(off + nbytes) // 2]
            return ap.bitcast(F32) if dt == F32 else ap

        def K8(ap):
            return ap.rearrange("p (k n) -> p k n", k=8)

        cv_t = sb("cvec_t", [128, 96], F32)
        wdw_t = sb("wdw_t", [128, 8 * 31], F32)
        hm_t = sb("hm_t", [128, 2], F32)
        ident_t = sb("ident_t", [128, 128], F32)
        dist3_t = sb("dist3_t", [128, 512], F32)
        onesm = sb("onesm", [128, 128], BF16)
        onesf = sb("onesf", [128, 64], F32)
        banks = [es.enter_context(nc.psum_tensor("bank%d" % i, [128, 512], F32)) for i in range(8)]

        sems = {e: es.enter_context(nc.semaphore("s_" + e)) for e in ENGS}
        S = Sched(nc, sems)
        NDS = {"sp": 40, "pool": 24}
        for q_, n_ in NDS.items():
            for i in range(n_):
                S.dma_sems["%s%d" % (q_, i)] = [es.enter_context(nc.semaphore("d%s%d" % (q_, i))), 0]
        dsi = {"sp": 0, "pool": 0}

        def DMA(eng, fn, reads=(), writes=()):
            sn = "%s%d" % (eng, dsi[eng] % NDS[eng])
            dsi[eng] += 1
            prev = S.dma_sems[sn][1]
            if prev > S.seen[eng].get(("dma", sn), 0):
                S.q[eng].append(("wait", None, [(("dma", sn), prev)], None))
                S.seen[eng][("dma", sn)] = prev
            return S.dma(eng, sn, fn, reads=reads, writes=writes)

        rr = {}

        def nbank(lo, hi):
            k = (lo, hi)
            i = lo + rr.get(k, 0) % (hi - lo)
            rr[k] = rr.get(k, 0) + 1
            return i

        C_BIN, C_BDW, C_LNG, C_LNB, C_BOUT, C_PG0, C_PB0, C_BBOUT, C_PG1, C_PB1 = 0, 24, 32, 40, 48, 56, 64, 72, 80, 88

        def col(c0, i):
            return cv_t[:, c0 + i:c0 + i + 1]

        def mm(out, lhsT, rhs, start, stop, reads, writes):
            S.op("pe", I("matmul", out=out, lhsT=lhsT, rhs=rhs, start=start, stop=stop), reads=reads, writes=writes, mark=stop)

        for t, src, nm in ((cv_t, cvec, "cvec"), (wdw_t, wdw, "wdw"), (hm_t, hm, "hm"), (ident_t, ident, "ident"), (dist3_t, dist3, "dist3")):
            DMA("sp", I("dma_start", out=t[:], in_=src), writes=[nm])
        S.op("dve", I("memset", ap=onesm[:], constant=1.0 / 1024.0), writes=["onesm"])
        S.op("dve", I("memset", ap=onesf[:], constant=1.0), writes=["onesf"])

        o = 0
        NDT = 4
        NPT = 31 - NDT
        diag = AR(o, NPT * 8 * 256).rearrange("p (j c m) -> p j c m", j=NPT, c=8); o += NPT * 8 * 256
        winb = K8(AR(o, 8 * 3072 * 2)); o += 8 * 3072 * 2
        woutb = K8(AR(o, 8 * 1024 * 2)); o += 8 * 1024 * 2
        xt = K8(AR(o, 8 * TA * 4, F32)); o += 8 * TA * 4
        xb = [K8(AR(o + i * 8 * TA * 2, 8 * TA * 2)) for i in range(2)]; o += 2 * 8 * TA * 2
        UW = PRE + TA
        ub = [K8(AR(o + i * 8 * UW * 2, 8 * UW * 2)) for i in range(2)]; o += 2 * 8 * UW * 2
        sg = [AR(o + i * TA * 4, TA * 4, F32) for i in range(2)]; o += 2 * TA * 4
        szb = [K8(AR(o + i * 8 * TA * 2, 8 * TA * 2)) for i in range(2)]; o += 2 * 8 * TA * 2
        AR_cvf0, AR_cvf1 = AR(o, 2048, F32), AR(o + 4096, 2048, F32)
        cvf = K8(AR(o, 8 * TA * 4, F32)); o += 8 * TA * 4
        cvb = K8(AR(o, 8 * TA * 2)); o += 8 * TA * 2
        sqb = K8(AR(o, 8 * TA * 2)); o += 8 * TA * 2
        rbb, rsq = cvb, sqb
        st1 = [AR(o + i * TA * 8, TA * 8, F32) for i in range(2)]; o += 2 * TA * 8
        st2 = [AR(o + i * TA * 8, TA * 8, F32) for i in range(2)]; o += 2 * TA * 8
        tt = [AR(o + i * TA * 8, TA * 8, F32) for i in range(4)]; o += 4 * TA * 8
        yy = [AR(o + i * TA * 4, TA * 4).rearrange("p (k n) -> p k n", k=2) for i in range(4)]; o += 4 * TA * 4
        vbs = [K8(AR(o + i * 8 * TA * 2, 8 * TA * 2)) for i in range(2)]; o += 2 * 8 * TA * 2
        rf = xt
        assert o <= NEL * 2, o

        def ln_stats(srcb, sqsrc, n, r_src, r_sq, bufs, tag, bm, be):
            mean_sb, rstd_sb = bufs
            for c in range(8):
                mm(banks[bm][:, 0:n], onesm[:], srcb[:, c, 0:n], c == 0, c == 7, ["onesm", r_src], ["bank%d" % bm])
            for c in range(8):
                mm(banks[be][:, 0:n], onesm[:], sqsrc[:, c, 0:n], c == 0, c == 7, ["onesm", r_sq], ["bank%d" % be])
            n2 = 2 * n
            S.op("act", I("activation", out=rstd_sb[:, 0:n], in_=banks[bm][:, 0:n], func=AF.Square), reads=["bank%d" % bm], writes=[tag + "rstd"])
            S.op("act", I("activation", out=mean_sb[:, 0:n], in_=banks[bm][:, 0:n], func=AF.Copy), reads=["bank%d" % bm], writes=[tag + "mean"])
            S.op("act", I("activation", out=mean_sb[:, n:n2], in_=banks[bm][:, 0:n], func=AF.Copy), reads=["bank%d" % bm], writes=[tag + "mean"])
            yield
            S.op("dve", I("scalar_tensor_tensor", out=rstd_sb[:, 0:n], in0=rstd_sb[:, 0:n], scalar=-1.0, in1=banks[be][:, 0:n], op0=ALU.mult, op1=ALU.add),
                 reads=["bank%d" % be, tag + "rstd"], writes=[tag + "rstd"])
            S.op("act", I("activation", out=rstd_sb[:, 0:n], in_=rstd_sb[:, 0:n], func=AF.Sqrt, bias=LN_EPS, scale=1.0), reads=[tag + "rstd"], writes=[tag + "rstd"])
            yield
            S.op("dve", I("reciprocal", out=rstd_sb[:, 0:n], in_=rstd_sb[:, 0:n]), reads=[tag + "rstd"], writes=[tag + "rstd"])
            S.op("dve", I("tensor_copy", out=rstd_sb[:, n:n2], in_=rstd_sb[:, 0:n]), reads=[tag + "rstd"], writes=[tag + "rstd"])

        def build_diag(c):
            for jj in range(NPT):
                j = NDT + jj
                sel = 1 if jj % 3 == 2 else 0
                res = "diag%d_%d" % (c, sel)
                if sel == 0:
                    S.op("dve", I("tensor_scalar", out=diag[:, jj, c, :], in0=ident_t[:], scalar1=wdw_t[:, c * 31 + j:c * 31 + j + 1], scalar2=None, op0=ALU.mult),
                         reads=["ident", "wdw"], writes=[res])
                else:
                    S.op("act", I("activation", out=diag[:, jj, c, :], in_=ident_t[:], func=AF.Copy, scale=wdw_t[:, c * 31 + j:c * 31 + j + 1]),
                         reads=["ident", "wdw"], writes=[res])

        def diag_gen():
            for c in range(4, 8):
                yield
                build_diag(c)
                yield

        stg = [(AR_cvf0, ["cvf0", "cvf1"]), (AR_cvf1, ["cvf4", "cvf5"]), (vbs[0].rearrange("p k n -> p (k n)").bitcast(F32), ["vb0"]),
               (vbs[1].rearrange("p k n -> p (k n)").bitcast(F32), ["vb1"]), (st1[0], ["s1mean"]), (st1[1], ["s1rstd"]),
               (st2[0], ["s2mean"]), (st2[1], ["s2rstd"]), (tt[0], ["tt0"]), (tt[1], ["tt1"])]
        stg_i = {"i": 0}

        def load_piece(dst3, src3, dst_res, n_k):
            w = dst3.shape[2]
            per = max(1, 512 // w)
            for k0 in range(0, n_k, per):
                kk = min(per, n_k - k0)
                i = stg_i["i"] % len(stg)
                stg_i["i"] += 1
                buf, names = stg[i]
                bv = buf[:, 0:kk * w].rearrange("p (k n) -> p k n", k=kk)
                DMA("sp", I("dma_start", out=bv, in_=src3[:, k0:k0 + kk, :]), writes=["stg%d" % i])
                if i % 2:
                    S.op("act", I("activation", out=dst3[:, k0:k0 + kk, :], in_=bv, func=AF.Copy), reads=["stg%d" % i] + names, writes=[dst_res])
                else:
                    S.op("dve", I("tensor_copy", out=dst3[:, k0:k0 + kk, :], in_=bv), reads=["stg%d" % i] + names, writes=[dst_res])

        def load_win_piece(part, c):
            cs = slice(part * 1024 + c * 128, part * 1024 + (c + 1) * 128)
            load_piece(winb[:, :, cs], a_w_in[:, :, cs].rearrange("k p n -> p k n"), "winb%d_%d" % (part, c), 8)

        def load_x(ti):
            if ti < 0:
                DMA("pool", I("dma_start", out=xb[1][:, :, 0:PRE], in_=xT[:, :, 0:PRE].rearrange("k p n -> p k n")), writes=["xb1"])
            else:
                c0 = PRE + ti * TA
                DMA("pool", I("dma_start", out=xb[ti % 2][:, :, :], in_=xT[:, :, c0:c0 + TA].rearrange("k p n -> p k n")), writes=["xb%d" % (ti % 2)])

        load_x(-1)
        load_x(0)

        def late_weights():
            for c in range(8):
                load_win_piece(2, c)
                yield
            for k in range(8):
                load_piece(woutb[:, k, :].rearrange("p (a n) -> p a n", a=2), a_w_out[k].rearrange("p (a n) -> p a n", a=2), "woutb", 2)
                yield

        diag_done = {"c": 0}

        def glu_chunk(c, n, xsrc, xres_, udst, ures, ucol0):
            bg = nbank(0, 6)
            for k in range(8):
                mm(banks[bg][:, 0:n], winb[:, k, 1024 + c * 128:1024 + (c + 1) * 128], xsrc[:, k, 0:n], k == 0, k == 7, ["winb1_%d" % c, xres_], ["bank%d" % bg])
            s = sg[c % 2]
            S.op("act", I("activation", out=s[:, 0:n], in_=banks[bg][:, 0:n], func=AF.Sigmoid, bias=col(C_BIN, 8 + c), scale=1.0),
                 reads=["bank%d" % bg, "cvec"], writes=["sg%d" % (c % 2)])
            ba = nbank(0, 6)
            for k in range(8):
                mm(banks[ba][:, 0:n], winb[:, k, c * 128:(c + 1) * 128], xsrc[:, k, 0:n], k == 0, k == 7, ["winb0_%d" % c, xres_], ["bank%d" % ba])
            S.op("dve", I("scalar_tensor_tensor", out=udst[:, c, ucol0:ucol0 + n], in0=banks[ba][:, 0:n], scalar=col(C_BIN, c), in1=s[:, 0:n], op0=ALU.add, op1=ALU.mult),
                 reads=["bank%d" % ba, "sg%d" % (c % 2), "cvec"], writes=[ures])

        for c in range(8):
            for part in (1, 0):
                load_win_piece(part, c)
            glu_chunk(c, PRE, xb[1], "xb1", ub[0], "u0", 0)
            if c % 2 == 1:
                build_diag(c // 2)
        S.op("pool", I("tensor_scalar", out=ub[0][:, :, 0:PRE], in0=ub[0][:, :, 0:PRE], scalar1=hm_t[:, 0:1], scalar2=None, op0=ALU.mult),
             reads=["u0", "hm"], writes=["u0"])

        def stage_Ag(ti):
            p = ti % 2
            U, Un = ub[p], ub[1 - p]
            X = xb[p]
            if ti + 1 < NTA:
                load_x(ti + 1)
            for c in range(8):
                glu_chunk(c, TA, X, "xb%d" % p, U, "u%d" % p, PRE)
                yield
            if ti + 1 < NTA:
                if ti + 1 == HALO // TA:
                    S.op("pool", I("tensor_scalar", out=Un[:, :, 0:PRE], in0=U[:, :, TA:TA + PRE], scalar1=hm_t[:, 1:2], scalar2=None, op0=ALU.mult),
                         reads=["u%d" % p, "hm"], writes=["u%d" % (1 - p)])
                else:
                    S.op("pool", I("tensor_copy", out=Un[:, :, 0:PRE], in_=U[:, :, TA:TA + PRE]), reads=["u%d" % p], writes=["u%d" % (1 - p)])

        def stage_Az(ti):
            p = ti % 2
            X = xb[p]
            for c in range(8):
                bz = nbank(0, 6)
                for k in range(8):
                    mm(banks[bz][:, 0:TA], winb[:, k, 2048 + c * 128:2048 + (c + 1) * 128], X[:, k, :], k == 0, k == 7, ["winb2_%d" % c, "xb%d" % p], ["bank%d" % bz])
                S.op("act", I("activation", out=szb[p][:, c, :], in_=banks[bz][:, 0:TA], func=AF.Silu, bias=col(C_BIN, 16 + c), scale=1.0),
                     reads=["bank%d" % bz, "cvec"], writes=["szb%d" % p])
                yield

        def stage_C(ti):
            p = ti % 2
            U = ub[p]
            for c in range(8):
                bc = nbank(0, 6)
                for jj in range(NPT):
                    j = NDT + jj
                    mm(banks[bc][:, 0:TA], diag[:, jj, c, :], U[:, c, 2 + j:2 + j + TA], jj == 0, jj == NPT - 1,
                       ["diag%d_%d" % (c, 1 if jj % 3 == 2 else 0), "u%d" % p], ["bank%d" % bc])
                S.op("act", I("activation", out=cvf[:, c, :], in_=banks[bc][:, 0:TA], func=AF.Identity, bias=col(C_BDW, c), scale=1.0),
                     reads=["bank%d" % bc, "cvec"], writes=["cvf%d" % c])
                for j in range(NDT):
                    S.op("dve", I("scalar_tensor_tensor", out=cvf[:, c, :], in0=U[:, c, 2 + j:2 + j + TA], scalar=wdw_t[:, c * 31 + j:c * 31 + j + 1], in1=cvf[:, c, :],
                                  op0=ALU.mult, op1=ALU.add),
                         reads=["u%d" % p, "wdw", "cvf%d" % c], writes=["cvf%d" % c])
            allc = ["cvf%d" % c for c in range(8)]
            S.op("act", I("activation", out=sqb[:, :, :], in_=cvf[:, :, :], func=AF.Square), reads=allc, writes=["sqb"])
            S.op("dve", I("tensor_copy", out=cvb[:, :, :], in_=cvf[:, :, :]), reads=allc, writes=["cvb"])

        def stage_S1(ti):
            p = ti % 2
            yield
            yield
            yield from ln_stats(cvb, sqb, TA, "cvb", "sqb", st1, "s1", 6, 7)
            mean_sb, rstd_sb = st1
            yield

            def gate(c2):
                y = yy[c2 % 4]
                S.op("dve", I("tensor_tensor", out=vbs[p][:, 2 * c2:2 * c2 + 2, :], in0=y[:, :, :], in1=szb[p][:, 2 * c2:2 * c2 + 2, :], op=ALU.mult),
                     reads=["yy%d" % (c2 % 4), "szb%d" % p], writes=["vb%d" % p])

            for c2 in range(4):
                t, y = tt[c2 % 4], yy[c2 % 4]
                tv = t.rearrange("p (k n) -> p k n", k=2)
                S.op("dve", I("tensor_tensor", out=tv, in0=cvf[:, 2 * c2:2 * c2 + 2, :], in1=mean_sb.rearrange("p (k n) -> p k n", k=2), op=ALU.subtract),
                     reads=["cvf%d" % (2 * c2), "cvf%d" % (2 * c2 + 1), "s1mean"], writes=["tt%d" % (c2 % 4)])
                S.op("dve", I("tensor_tensor", out=t[:], in0=t[:], in1=rstd_sb[:], op=ALU.mult), reads=["tt%d" % (c2 % 4), "s1rstd"], writes=["tt%d" % (c2 % 4)])
                for j in range(2):
                    c = 2 * c2 + j
                    S.op("act", I("activation", out=y[:, j, :], in_=tv[:, j, :], func=AF.Silu, bias=col(C_LNB, c), scale=col(C_LNG, c)),
                         reads=["tt%d" % (c2 % 4), "cvec"], writes=["yy%d" % (c2 % 4)])
                if c2 >= 1:
                    gate(c2 - 1)
                yield
            gate(3)

        RF = ["rf0", "rf1", "rf2", "rf3"]

        def stage_O(ti):
            c0 = PRE + ti * TA
            DMA("sp", I("dma_start", out=xt[:, :, :], in_=xT[:, :, c0:c0 + TA].rearrange("k p n -> p k n")), writes=RF)
            S.op("dve", I("tensor_scalar", out=xt[:, :, :], in0=xt[:, :, :], scalar1=ALPHA, scalar2=None, op0=ALU.mult), reads=RF, writes=RF)
            for m in range(8):
                bo = nbank(0, 6)
                for c in range(8):
                    mm(banks[bo][:, 0:TA], woutb[:, c, m * 128:(m + 1) * 128], vbs[ti % 2][:, c, :], c == 0, c == 7, ["woutb", "vb%d" % (ti % 2)], ["bank%d" % bo])
                S.op("dve", I("scalar_tensor_tensor", out=rf[:, m, :], in0=banks[bo][:, 0:TA], scalar=col(C_BOUT, m), in1=xt[:, m, :], op0=ALU.add, op1=ALU.add),
                     reads=["bank%d" % bo, "rf%d" % (m // 2), "cvec"], writes=["rf%d" % (m // 2)])
                yield
            S.op("dve", I("tensor_copy", out=rbb[:, :, :], in_=rf[:, :, :]), reads=RF, writes=["cvb"])
            S.op("act", I("activation", out=rsq[:, :, :], in_=rf[:, :, :], func=AF.Square), reads=RF, writes=["sqb"])

        def stage_S2(ti):
            own = ti >= HALO // TA
            yield from ln_stats(rbb, rsq, TA, "cvb", "sqb", st2, "s2", 6, 7)
            mean_sb, rstd_sb = st2
            yield
            def affine(c2):
                tv = tt[c2 % 4].rearrange("p (k n) -> p k n", k=2)
                for j in range(2):
                    m = 2 * c2 + j
                    dst = rf[:, m, :] if own else rbb[:, m, :]
                    S.op("act", I("activation", out=dst, in_=tv[:, j, :], func=AF.Identity, bias=col(C_PB0, m), scale=col(C_PG0, m)),
                         reads=["tt%d" % (c2 % 4), "cvec"], writes=["rf%d" % c2 if own else "cvb"])

            for c2 in range(4):
                t = tt[c2 % 4]
                tv = t.rearrange("p (k n) -> p k n", k=2)
                S.op("dve", I("tensor_tensor", out=tv, in0=rf[:, 2 * c2:2 * c2 + 2, :], in1=mean_sb.rearrange("p (k n) -> p k n", k=2), op=ALU.subtract),
                     reads=["rf%d" % c2, "s2mean"], writes=["tt%d" % (c2 % 4)])
                S.op("dve", I("tensor_tensor", out=t[:], in0=t[:], in1=rstd_sb[:], op=ALU.mult), reads=["tt%d" % (c2 % 4), "s2rstd"], writes=["tt%d" % (c2 % 4)])
                if c2 >= 1:
                    affine(c2 - 1)
                yield
            affine(3)
            yield
            pos0 = ti * TA
            if own:
                S.op("dve", I("tensor_copy", out=rbb[:, :, :], in_=rf[:, :, :]), reads=RF, writes=["cvb"])
                p1 = pos0 - HALO
                DMA("sp", I("dma_start", out=x1f[:, :, p1:p1 + TA].rearrange("k p n -> p k n"), in_=rf[:, :, :]), reads=RF, writes=["x1f"])
            DMA("sp", I("dma_start", out=x1T_d[:, :, pos0:pos0 + TA].rearrange("k p n -> p k n"), in_=rbb[:, :, :]), reads=["cvb"], writes=["x1T_d"])

        def run(*gens):
            gens = list(gens)
            while gens:
                for g in list(gens):
                    try:
                        next(g)
                    except StopIteration:
                        gens.remove(g)

        run(stage_Ag(0), late_weights(), diag_gen())
        run(stage_Az(0))
        for i in range(NTA + 1):
            if i < NTA:
                stage_C(i)
            g = []
            if i >= 1:
                g.append(stage_O(i - 1))
            if i < NTA:
                g.append(stage_S1(i))
            if i + 1 < NTA:
                g.append(stage_Az(i + 1))
            run(*g)
            g = []
            if i + 1 < NTA:
                g.append(stage_Ag(i + 1))
            if i >= 1:
                g.append(stage_S2(i - 1))
            run(*g)

        S.barrier()

        NT = HALO + OWN
        o = 0
        x1T = K8(AR(o, 8 * NT * 2)); o += 8 * NT * 2
        gT = K8(AR(o, 8 * OWN * 2)); o += 8 * OWN * 2
        acc = [AR(o + i * OWN * 4, OWN * 4, F32) for i in range(2)]; o += 2 * OWN * 4
        szhs = [AR(o + i * OWN * 4, OWN * 4, F32) for i in range(2)]; o += 2 * OWN * 4
        o_attn = o
        wsl = [AR(o + i * 20480, 20480).rearrange("p (s k n) -> p s k n", s=10, k=8) for i in range(2)]; o += 2 * 20480
        qd = AR(o, OWN * 2); o += OWN * 2
        kd = AR(o, NT * 2); o += NT * 2
        vaug = AR(o, 32 * 2 * 65 * 2).rearrange("p (b h e) -> p b h e", b=32, h=2); o += 32 * 2 * 65 * 2
        D2 = AR(o, 2 * 2 * 512 * 2).rearrange("p (h v n) -> p h v n", h=2, v=2); o += 2 * 2 * 512 * 2
        eb = [AR(o + i * 1024, 1024) for i in range(2)]; o += 2 * 1024
        pb = [AR(o + i * 1024, 1024) for i in range(3)]; o += 3 * 1024
        tnb = AR(o, 512 * 4, F32); o += 512 * 4
        omax = o
        o = o_attn
        woutB = K8(AR(o, 8 * 1024 * 2)); o += 8 * 1024 * 2
        xall = K8(AR(0, 8 * OWN * 4, F32))
        rB = [K8(AR(o + i * 8 * TA * 4, 8 * TA * 4, F32)) for i in range(3)]; o += 3 * 8 * TA * 4
        rBb = [K8(AR(o + i * 8 * TA * 2, 8 * TA * 2)) for i in range(2)]; o += 2 * 8 * TA * 2
        rBq = [K8(AR(o + i * 8 * TA * 2, 8 * TA * 2)) for i in range(2)]; o += 2 * 8 * TA * 2
        st3 = [AR(o + i * TA * 8, TA * 8, F32) for i in range(2)]; o += 2 * TA * 8
        o_dead = 8 * NT * 2 + 8 * OWN * 2
        ttB = [AR(o_dead + i * TA * 8, TA * 8, F32) for i in range(4)]
        stgB = [AR(o_dead + 4 * TA * 8 + i * 2048, 2048, F32) for i in range(8)]
        assert max(o, omax) <= NEL * 2, (o, omax)

        for k in range(8):
            pass
        def xblocks(a0, n, step=1):
            last = a0 + (n - 1) * step
            return ["x1T_b%d" % b for b in range(a0 // 1024, last // 1024 + 1)]
        for blk in (2, 3, 0, 1):
            cs_ = slice(blk * 1024, (blk + 1) * 1024)
            DMA("sp", I("dma_start", out=x1T[:, :, cs_], in_=x1T_d[:, :, cs_].rearrange("k p n -> p k n")), reads=["x1T_d"], writes=["x1T_b%d" % blk])
        S.op("dve", I("memset", ap=vaug[:, :, :, 64:65], constant=1.0), writes=["vaug"])

        slopes = [2.0 ** (-8.0 * (h + 1) / 16.0) for h in range(16)]

        def load_w(hp):
            W = wsl[hp % 2]
            srcs = []
            for g in range(3):
                srcs.append((b_w_in, g * 1024 + hp * 128))
            for g in range(3):
                srcs.append((kv_w, g * 1024 + hp * 128))
            for g in range(3):
                srcs.append((kv_w, 3072 + g * 1024 + hp * 128))
            srcs.append((b_w_in, 3072 + hp * 128))
            order = [9, 0, 3, 6, 1, 4, 7, 2, 5, 8]
            for s in order:
                wsrc, c0 = srcs[s]
                DMA("pool", I("dma_start", out=W[:, s, :, :], in_=wsrc[:, :, c0:c0 + 128].rearrange("k p n -> p k n")), writes=["wsl%d_%d" % (hp % 2, s)])

        def norm_a(hp):
            for h2 in range(2):
                A = acc[h2]
                S.op("act", I("activation", out=A[64:65, :], in_=A[64:65, :], func=AF.Ln), reads=["acc%d" % h2], writes=["acc%d" % h2])
            for h2 in range(2):
                A = acc[h2]
                S.op("act", I("activation", out=A[64:65, :], in_=A[64:65, :], func=AF.Exp, scale=-1.0), reads=["acc%d" % h2], writes=["acc%d" % h2])

        def norm_b(hp):
            szh = szhs[hp % 2]
            for h2 in range(2):
                A = acc[h2]
                pl = slice(64 * h2, 64 * h2 + 64)
                for t4 in range(OWN // 512):
                    cs = slice(t4 * 512, (t4 + 1) * 512)
                    bb = nbank(0, 3)
                    S.op("pe", I("matmul", out=banks[bb][0:64, :], lhsT=onesf[64:65, 0:64], rhs=A[64:65, cs], start=True, stop=True),
                         reads=["onesf", "acc%d" % h2], writes=["bank%d" % bb])
                    S.op("dve", I("tensor_tensor", out=tnb[pl, :], in0=A[0:64, cs], in1=banks[bb][0:64, :], op=ALU.mult), reads=["acc%d" % h2, "bank%d" % bb], writes=["tnb"])
                    S.op("dve", I("tensor_tensor", out=gT[pl, hp, cs], in0=tnb[pl, :], in1=szh[pl, cs], op=ALU.mult), reads=["tnb", "szh%d" % (hp % 2)], writes=["gT"])

        load_w(0)
        for hp in range(8):
            W = wsl[hp % 2]
            wr = lambda s: "wsl%d_%d" % (hp % 2, s)
            for t4 in range(OWN // 512):
                bz = nbank(0, 3)
                for k in range(8):
                    mm(banks[bz][:, :], W[:, 9, k, :], x1T[:, k, HALO + t4 * 512:HALO + (t4 + 1) * 512], k == 0, k == 7, [wr(9)] + xblocks(HALO + t4 * 512, 512), ["bank%d" % bz])
                S.op("act", I("activation", out=szhs[hp % 2][:, t4 * 512:(t4 + 1) * 512], in_=banks[bz][:, :], func=AF.Silu), reads=["bank%d" % bz], writes=["szh%d" % (hp % 2)])
            if hp + 1 < 8:
                load_w(hp + 1)
            for g, (window, d) in enumerate(GROUPS):
                halo = 128 * d
                nq = 16 // d
                qv = qd.rearrange("p (r n i) -> p r n i", r=d, n=nq)
                kv_ = kd[:, 0:d * (nq + 1) * 128].rearrange("p (r n i) -> p r n i", r=d, n=nq + 1)
                for t4 in range(OWN // 512):
                    bq = nbank(0, 3)
                    for k in range(8):
                        mm(banks[bq][:, :], W[:, g, k, :], x1T[:, k, HALO + t4 * 512:HALO + (t4 + 1) * 512], k == 0, k == 7, [wr(g)] + xblocks(HALO + t4 * 512, 512), ["bank%d" % bq])
                    L = 512 // d
                    l0 = t4 * L
                    nb_, i0 = l0 // 128, l0 % 128
                    if d == 1:
                        dst, src = qd[:, t4 * 512:(t4 + 1) * 512], banks[bq][:, :]
                    else:
                        dst, src = qv[:, :, nb_, i0:i0 + L], banks[bq][:, :].rearrange("p (l r) -> p r l", r=d)
                    S.op("act", I("activation", out=dst, in_=src, func=AF.Copy, scale=0.125), reads=["bank%d" % bq], writes=["qd"])
                ntok = halo + OWN
                for t0 in range(0, ntok, 512):
                    n = min(512, ntok - t0)
                    bk = nbank(0, 3)
                    a0 = HALO - halo + t0
                    for k in range(8):
                        mm(banks[bk][:, 0:n], W[:, 3 + g, k, :], x1T[:, k, a0:a0 + n], k == 0, k == 7, [wr(3 + g)] + xblocks(a0, n), ["bank%d" % bk])
                    L = n // d
                    l0 = t0 // d
                    nb_, i0 = l0 // 128, l0 % 128
                    if d == 1:
                        dst, src = kd[:, t0:t0 + n], banks[bk][:, 0:n]
                    else:
                        dst, src = kv_[:, :, nb_, i0:i0 + L], banks[bk][:, 0:n].rearrange("p (l r) -> p r l", r=d)
                    S.op("dve", I("tensor_copy", out=dst, in_=src), reads=["bank%d" % bk], writes=["kd"])
                nblk = d * (nq + 1)
                for blk in range(nblk):
                    r, b = blk // (nq + 1), blk % (nq + 1)
                    bv = nbank(0, 3)
                    a0 = HALO - halo + d * b * 128 + r
                    for k in range(8):
                        mm(banks[bv][:, 0:128], x1T[:, k, a0:a0 + 127 * d + 1:d], W[:, 6 + g, k, :], k == 0, k == 7, [wr(6 + g)] + xblocks(a0, 128, d), ["bank%d" % bv])
                    src = banks[bv][:, 0:128].rearrange("p (h e) -> p h e", h=2)
                    if blk % 3 == 2:
                        S.op("act", I("activation", out=vaug[:, blk, :, 0:64], in_=src, func=AF.Copy), reads=["bank%d" % bv], writes=["vaug"])
                    else:
                        S.op("dve", I("tensor_copy", out=vaug[:, blk, :, 0:64], in_=src), reads=["bank%d" % bv], writes=["vaug"])
                for h2 in range(2):
                    sc = -slopes[hp * 2 + h2] * d
                    df_, dn_ = dist3_t[:, 0:256], dist3_t[:, 256:512]
                    S.op("act", I("activation", out=D2[:, h2, 0, 0:256], in_=df_, func=AF.Exp, scale=sc), reads=["dist3"], writes=["D2"])
                    S.op("act", I("activation", out=D2[:, h2, 0, 256:512], in_=(df_ if d == 16 else dn_), func=AF.Exp, scale=sc), reads=["dist3"], writes=["D2"])
                    if d != 16:
                        S.op("act", I("activation", out=D2[:, h2, 1, 0:256], in_=dn_, func=AF.Exp, scale=sc), reads=["dist3"], writes=["D2"])
                        S.op("act", I("activation", out=D2[:, h2, 1, 256:512], in_=dn_, func=AF.Exp, scale=sc), reads=["dist3"], writes=["D2"])
                if g == 0 and hp >= 1:
                    norm_b(hp - 1)
                pairs = [(h2, P) for h2 in range(2) for P in range(8)]

                def kq(h2, u):
                    pl = slice(64 * h2, 64 * h2 + 64)
                    r, n = u // nq, u % nq
                    if d == 1:
                        return kd[pl, n * 128:(n + 1) * 128], kd[pl, (n + 1) * 128:(n + 2) * 128], qd[pl, n * 128:(n + 1) * 128], n
                    return kv_[pl, r, n, :], kv_[pl, r, n + 1, :], qv[pl, r, n, :], r * (nq + 1) + n

                def emit_qk(idx):
                    h2, P = pairs[idx]
                    bs = 3 + idx % 3
                    for j in range(2):
                        kp, kc, q, _ = kq(h2, 2 * P + j)
                        mm(banks[bs][:, j * 256:j * 256 + 128], kp, q, True, True, ["kd", "qd"], ["bank%d" % bs])
                        mm(banks[bs][:, j * 256 + 128:j * 256 + 256], kc, q, True, True, ["kd", "qd"], ["bank%d" % bs])
                    E, Pb = eb[idx % 2], pb[idx % 3]
                    S.op("act", I("activation", out=E[:, :], in_=banks[bs][:, :], func=AF.Exp), reads=["bank%d" % bs], writes=["eb%d" % (idx % 2)])
                    var = 0 if (d == 16 or (2 * P) % nq == 0) else 1
                    S.op("dve", I("tensor_tensor", out=Pb[:, :], in0=E[:, :], in1=D2[:, h2, var, :], op=ALU.mult), reads=["eb%d" % (idx % 2), "D2"], writes=["pb%d" % (idx % 3)])

                def emit_pv(idx):
                    h2, P = pairs[idx]
                    Pb = pb[idx % 3]
                    bo = 6 + (P // 2) % 2
                    A = acc[h2]
                    for j in range(2):
                        u = 2 * P + j
                        _, _, _, blk0 = kq(h2, u)
                        oc = (u % 4) * 128
                        mm(banks[bo][0:65, oc:oc + 128], vaug[:, blk0, h2, :], Pb[:, j * 256:j * 256 + 128], True, False, ["vaug", "pb%d" % (idx % 3)], ["bank%d" % bo])
                        mm(banks[bo][0:65, oc:oc + 128], vaug[:, blk0 + 1, h2, :], Pb[:, j * 256 + 128:j * 256 + 256], False, True, ["vaug", "pb%d" % (idx % 3)], ["bank%d" % bo])
                    if P % 2 == 1:
                        k4 = P // 2
                        if d == 1:
                            av, pv = A[0:65, k4 * 512:(k4 + 1) * 512], banks[bo][0:65, :]
                        elif d == 4:
                            av, pv = A[0:65, :].rearrange("p (l r) -> p r l", r=4)[:, k4, :], banks[bo][0:65, :]
                        else:
                            av = A[0:65, :].rearrange("p (i r) -> p r i", r=16)[:, 4 * k4:4 * k4 + 4, :]
                            pv = banks[bo][0:65, :].rearrange("p (r i) -> p r i", r=4)
                        if g == 0:
                            S.op("act", I("activation", out=av, in_=pv, func=AF.Copy), reads=["bank%d" % bo], writes=["acc%d" % h2])
                        else:
                            S.op("dve", I("tensor_tensor", out=av, in0=pv, in1=av, op=ALU.add), reads=["bank%d" % bo, "acc%d" % h2], writes=["acc%d" % h2])

                LA = 2
                for idx in range(len(pairs) + LA):
                    if idx >= LA:
                        emit_pv(idx - LA)
                    if idx < len(pairs):
                        emit_qk(idx)
            norm_a(hp)
        norm_b(7)

        S.barrier()
        for k in range(8):
            for a in range(2):
                i = (2 * k + a) % 8
                DMA("sp", I("dma_start", out=stgB[i][:, :], in_=b_w_out[k][:, a * 512:(a + 1) * 512]), writes=["stgB%d" % i])
                if a:
                    S.op("act", I("activation", out=woutB[:, k, a * 512:(a + 1) * 512], in_=stgB[i][:, :], func=AF.Copy), reads=["stgB%d" % i], writes=["woutB"])
                else:
                    S.op("dve", I("tensor_copy", out=woutB[:, k, a * 512:(a + 1) * 512], in_=stgB[i][:, :]), reads=["stgB%d" % i], writes=["woutB"])
        NTB = OWN // TA

        for k in range(8):
            DMA("sp", I("dma_start", out=xall[:, k, :], in_=x1f[k]), reads=["x1f"], writes=["xall%d" % k])
            S.op("act", I("activation", out=xall[:, k, :], in_=xall[:, k, :], func=AF.Copy, scale=ALPHA), reads=["xall%d" % k], writes=["xall%d" % k])

        def stage_OB(ti):
            p = ti % 2
            p3 = ti % 3
            cs = slice(ti * TA, (ti + 1) * TA)
            for m in range(8):
                bo = nbank(0, 6)
                for c in range(8):
                    mm(banks[bo][:, 0:TA], woutB[:, c, m * 128:(m + 1) * 128], gT[:, c, cs], c == 0, c == 7, ["woutB", "gT"], ["bank%d" % bo])
                S.op("dve", I("scalar_tensor_tensor", out=rB[p3][:, m, :], in0=banks[bo][:, 0:TA], scalar=col(C_BBOUT, m), in1=xall[:, m, cs], op0=ALU.add, op1=ALU.add),
                     reads=["bank%d" % bo, "xall%d" % m, "cvec"], writes=["rB%d_%d" % (p3, m // 2)])
                if m % 2 == 1:
                    q_ = m // 2
                    S.op("dve", I("tensor_copy", out=rBb[p][:, m - 1:m + 1, :], in_=rB[p3][:, m - 1:m + 1, :]), reads=["rB%d_%d" % (p3, q_)], writes=["rBb%d" % p])
                    S.op("act", I("activation", out=rBq[p][:, m - 1:m + 1, :], in_=rB[p3][:, m - 1:m + 1, :], func=AF.Square), reads=["rB%d_%d" % (p3, q_)], writes=["rBq%d" % p])
                yield

        def stage_SB(ti):
            p = ti % 2
            p3 = ti % 3
            cs = slice(ti * TA, (ti + 1) * TA)
            yield
            yield
            yield from ln_stats(rBb[p], rBq[p], TA, "rBb%d" % p, "rBq%d" % p, st3, "s3", 6, 7)
            mean_sb, rstd_sb = st3
            yield

            def affine(c2):
                tv = ttB[c2 % 4].rearrange("p (k n) -> p k n", k=2)
                for j in range(2):
                    m = 2 * c2 + j
                    S.op("act", I("activation", out=rB[p3][:, m, :], in_=tv[:, j, :], func=AF.Identity, bias=col(C_PB1, m), scale=col(C_PG1, m)),
                         reads=["ttB%d" % (c2 % 4), "cvec"], writes=["rB%d_%d" % (p3, c2)])

            for c2 in range(4):
                t = ttB[c2 % 4]
                tv = t.rearrange("p (k n) -> p k n", k=2)
                S.op("dve", I("tensor_tensor", out=tv, in0=rB[p3][:, 2 * c2:2 * c2 + 2, :], in1=mean_sb.rearrange("p (k n) -> p k n", k=2), op=ALU.subtract),
                     reads=["rB%d_%d" % (p3, c2), "s3mean"], writes=["ttB%d" % (c2 % 4)])
                S.op("dve", I("tensor_tensor", out=t[:], in0=t[:], in1=rstd_sb[:], op=ALU.mult), reads=["ttB%d" % (c2 % 4), "s3rstd"], writes=["ttB%d" % (c2 % 4)])
                if c2 >= 1:
                    affine(c2 - 1)
                yield
            affine(3)
            DMA("sp", I("dma_start", out=outT[:, :, cs].rearrange("k p n -> p k n"), in_=rB[p3][:, :, :]), reads=["rB%d_%d" % (p3, q_) for q_ in range(4)], writes=["outT"])

        run(stage_OB(0))
        for ti in range(NTB):
            g = [stage_SB(ti)]
            if ti + 1 < NTB:
                g.insert(0, stage_OB(ti + 1))
            run(*g)
        S.barrier()

        with nc.Block() as block:
            @block.sync
            def _(e):
                S.replay("sp", e)

            @block.scalar
            def _(e):
                S.replay("act", e)

            @block.vector
            def _(e):
                S.replay("dve", e)

            @block.gpsimd
            def _(e):
                S.replay("pool", e)

            @block.tensor
            def _(e):
                S.replay("pe", e)
    return nc


def _host_inputs(x, a_w_in, a_b_in, a_w_dw, a_b_dw, a_ln_g, a_ln_b, a_w_out, a_b_out,
                 kv_w, b_w_in, b_w_out, b_b_out, post_ln_g, post_ln_b):
    f = np.float32
    x2 = np.asarray(x, f)[0]
    xpad = np.concatenate([np.zeros((PRE + HALO, D), f), x2], axis=0)

    def pcol(v):
        v = np.asarray(v, f).reshape(-1, 128)
        return v.T

    cvec = np.concatenate([pcol(a_b_in[0]), pcol(a_b_dw[0]), pcol(a_ln_g[0]), pcol(a_ln_b[0]), pcol(a_b_out[0]),
                           pcol(post_ln_g[0]), pcol(post_ln_b[0]), pcol(b_b_out[0]), pcol(post_ln_g[1]), pcol(post_ln_b[1])], axis=1)
    assert cvec.shape == (128, 96)
    wdw = np.asarray(a_w_dw, f)[0].reshape(31, 8, 128).transpose(2, 1, 0).reshape(128, 8 * 31)
    j = np.arange(128)[:, None]
    qi = np.arange(128)[None, :]
    prev = np.where(j >= qi, (qi + 128 - j).astype(f), f(BIG))
    cur = np.where(j <= qi, (qi - j).astype(f), f(BIG))
    distn = np.concatenate([prev, cur], axis=1).astype(f)
    common = {
        "a_w_in": np.ascontiguousarray(np.asarray(a_w_in, f)[0].reshape(8, 128, 3072)),
        "a_w_out": np.ascontiguousarray(np.asarray(a_w_out, f)[0].reshape(8, 128, 1024)),
        "kv_w": np.ascontiguousarray(np.asarray(kv_w, f).reshape(8, 128, 6144)),
        "b_w_in": np.ascontiguousarray(np.asarray(b_w_in, f)[0].reshape(8, 128, 4096)),
        "b_w_out": np.ascontiguousarray(np.asarray(b_w_out, f)[0].reshape(8, 128, 1024)),
        "cvec": np.ascontiguousarray(cvec),
        "wdw": np.ascontiguousarray(wdw),
        "ident": np.eye(128, dtype=f),
    }
    bigp = np.full((128, 128), BIG, f)
    first0 = np.concatenate([bigp, cur], axis=1).astype(f)
    in_maps = []
    for c in range(NCORES):
        xs = xpad[OWN * c:OWN * c + XCOLS]
        m = dict(common)
        m["xT"] = np.ascontiguousarray(xs.T.reshape(8, 128, XCOLS))
        hmv = np.zeros((128, 2), f)
        hmv[:, 0] = 0.0 if c <= 1 else 1.0
        hmv[:, 1] = 0.0 if c == 0 else 1.0
        m["hm"] = hmv
        fu = first0 if c == 0 else distn
        m["dist3"] = np.ascontiguousarray(np.concatenate([fu, distn], axis=1))
        in_maps.append(m)
    return in_maps


_NC = {}


def kernel(**inputs):
    in_maps = _host_inputs(**inputs)
    if "nc" not in _NC:
        _NC["nc"] = build_nc()
    res = run_bass_kernel_spmd(_NC["nc"], in_maps, core_ids=list(range(NCORES)))
    outs = [np.asarray(r["outT"], np.float32).reshape(D, OWN).T for r in res.results]
    if DEBUG:
        kernel.dbg = [np.asarray(r["x1f"], np.float32).reshape(D, OWN).T for r in res.results]
    return np.concatenate(outs, axis=0)[None].astype(np.float32)
```
